# Optimizing a Trainium2 kernel written in Bass

```python
import jax, jax.numpy as jnp
from jax import lax
import numpy as np

D_MODEL = 1024
BATCH = 2
SEQ = 8192
DEPTH = 2

HGRN_WIDTH = D_MODEL
HGRN_HEADS = 8
HGRN_EXPAND = HGRN_WIDTH // HGRN_HEADS
HGRN_HEAD_V = HGRN_WIDTH // HGRN_HEADS
HGRN_FDIM = HGRN_HEADS * HGRN_EXPAND
GLA_HEADS = 4
GLA_KEY = D_MODEL // 2
GLA_VAL = D_MODEL
GLA_HEAD_K = GLA_KEY // GLA_HEADS
GLA_HEAD_V = GLA_VAL // GLA_HEADS
GLA_GATE_RANK = 16
GLA_GATE_TAU = 16.0
N_BRANCH = 2
CHUNK = 64
EPS = 1e-6
F_FLOOR = 1e-20
IN_SPLITS = (HGRN_FDIM, HGRN_FDIM, HGRN_WIDTH, HGRN_WIDTH,
             GLA_KEY, GLA_KEY, GLA_VAL, GLA_VAL,
             GLA_GATE_RANK,
             N_BRANCH * D_MODEL)
D_IN = sum(IN_SPLITS)
IN_OFFSETS = tuple(int(v) for v in np.cumsum(IN_SPLITS)[:-1])

kernel_name = "hybrid_hgrn2_gla_adaln"


def rms_norm(x, w):
    xf = x.astype(jnp.float32)
    y = xf * lax.rsqrt(jnp.mean(xf * xf, axis=-1, keepdims=True) + EPS)
    return (y * w.astype(jnp.float32)).astype(x.dtype)


def chunked_gated_linear_attention(q, k, v, log_a, scale):
    B, T, H, K = q.shape
    V = v.shape[-1]
    n = T // CHUNK
    f32 = jnp.float32

    def to_chunks(t):
        return t.astype(f32).reshape(B, n, CHUNK, H, t.shape[-1]).transpose(1, 0, 3, 2, 4)

    qc, kc, vc, gc = to_chunks(q * scale), to_chunks(k), to_chunks(v), to_chunks(log_a)
    causal = jnp.tril(jnp.ones((CHUNK, CHUNK), dtype=bool))[:, :, None]

    def step(S, inp):
        qi, ki, vi, gi = inp
        b = jnp.cumsum(gi, axis=2)
        o_inter = jnp.einsum('bhck,bhkv->bhcv', qi * jnp.exp(b), S)
        diff = b[:, :, :, None, :] - b[:, :, None, :, :]
        decay = jnp.where(causal, jnp.exp(jnp.minimum(diff, 0.0)), 0.0)
        scores = jnp.einsum('bhtk,bhsk,bhtsk->bhts', qi, ki, decay)
        o_intra = jnp.einsum('bhts,bhsv->bhtv', scores, vi)
        b_last = b[:, :, -1:, :]
        S_new = S * jnp.exp(b_last[:, :, 0, :])[..., None] + jnp.einsum(
            'bhsk,bhsv->bhkv', ki * jnp.exp(b_last - b), vi)
        return S_new, o_inter + o_intra

    S0 = jnp.zeros((B, H, K, V), f32)
    _, o = lax.scan(step, S0, (qc, kc, vc, gc))
    return o.transpose(1, 0, 3, 2, 4).reshape(B, T, H, V).astype(v.dtype)


def hgrn2_branch(hq, hf, hi, hz, lb, norm_w):
    B, T, _ = hq.shape
    q = jax.nn.silu(hq).reshape(B, T, HGRN_HEADS, HGRN_EXPAND)
    fr = hf.astype(jnp.float32).reshape(B, T, HGRN_HEADS, HGRN_EXPAND)
    lb = jnp.clip(lb.astype(jnp.float32), 0.0, 1.0).reshape(HGRN_HEADS, HGRN_EXPAND)
    f = lb + (1.0 - lb) * jax.nn.sigmoid(fr)
    log_f = jnp.log(jnp.maximum(f, F_FLOOR))
    k = (1.0 - lb) * jax.nn.sigmoid(-fr)
    v = hi.reshape(B, T, HGRN_HEADS, HGRN_HEAD_V)
    o = chunked_gated_linear_attention(q, k, v, log_f, 1.0)
    o = rms_norm(o.reshape(B, T, HGRN_WIDTH), norm_w)
    return o * jax.nn.silu(hz)


def gla_branch(gq, gk, gv, gz, ga, alpha_w, alpha_b, norm_w):
    B, T, _ = gq.shape
    q = gq.reshape(B, T, GLA_HEADS, GLA_HEAD_K)
    k = gk.reshape(B, T, GLA_HEADS, GLA_HEAD_K)
    v = gv.reshape(B, T, GLA_HEADS, GLA_HEAD_V)
    log_a = jax.nn.log_sigmoid((ga @ alpha_w + alpha_b).astype(jnp.float32)) / GLA_GATE_TAU
    log_a = log_a.reshape(B, T, GLA_HEADS, GLA_HEAD_K)
    o = chunked_gated_linear_attention(q, k, v, log_a, GLA_HEAD_K ** -0.5)
    o = rms_norm(o, norm_w)
    return o.reshape(B, T, GLA_VAL) * jax.nn.silu(gz)


def setup_inputs(seed: int = 0) -> dict:
    key = jax.random.key(seed)
    ks = jax.random.split(key, 16)
    f32 = jnp.float32
    nrm = lambda k, shape, s: jax.random.normal(k, shape, f32) * s
    return {
        "x": nrm(ks[0], (BATCH, SEQ, D_MODEL), 1.0),
        "c": nrm(ks[1], (BATCH, D_MODEL), 1.0),
        "ada_w": nrm(ks[2], (DEPTH, D_MODEL, 3 * D_MODEL), 0.5 * D_MODEL ** -0.5),
        "ada_b": nrm(ks[3], (DEPTH, 3 * D_MODEL), 0.01),
        "norm_w": 1.0 + nrm(ks[4], (DEPTH, D_MODEL), 0.02),
        "w_in": nrm(ks[5], (DEPTH, D_MODEL, D_IN), D_MODEL ** -0.5),
        "hgrn_lb_logits": nrm(ks[6], (DEPTH, HGRN_FDIM), 0.5),
        "hgrn_norm_w": 1.0 + nrm(ks[7], (DEPTH, HGRN_WIDTH), 0.02),
        "gla_alpha_w": nrm(ks[8], (DEPTH, GLA_GATE_RANK, GLA_KEY), GLA_GATE_RANK ** -0.5),
        "gla_alpha_b": nrm(ks[9], (DEPTH, GLA_KEY), 0.1),
        "gla_norm_w": 1.0 + nrm(ks[10], (DEPTH, GLA_HEAD_V), 0.02),
        "w_branch": nrm(ks[11], (DEPTH, N_BRANCH, HGRN_WIDTH, D_MODEL), HGRN_WIDTH ** -0.5),
        "w_out": nrm(ks[12], (DEPTH, D_MODEL, D_MODEL), D_MODEL ** -0.5),
        "final_norm_w": 1.0 + nrm(ks[13], (D_MODEL,), 0.02),
    }


def reference(x, c, ada_w, ada_b, norm_w, w_in, hgrn_lb_logits, hgrn_norm_w, gla_alpha_w,
              gla_alpha_b, gla_norm_w, w_branch, w_out, final_norm_w):
    B, T, _ = x.shape
    c_act = jax.nn.silu(c)
    p = jax.nn.softmax(hgrn_lb_logits.astype(jnp.float32), axis=0)
    lb_all = jnp.cumsum(p, axis=0) - p[0:1]
    for l in range(DEPTH):
        mod = c_act @ ada_w[l] + ada_b[l]
        shift, scale, gate = jnp.split(mod, 3, axis=-1)
        h = rms_norm(x, norm_w[l]) * (1.0 + scale[:, None, :]) + shift[:, None, :]
        proj = h @ w_in[l]
        hq, hf, hi, hz, gq, gk, gv, gz, ga, mg = jnp.split(proj, IN_OFFSETS, axis=-1)
        y_h = hgrn2_branch(hq, hf, hi, hz, lb_all[l], hgrn_norm_w[l])
        y_g = gla_branch(gq, gk, gv, gz, ga, gla_alpha_w[l], gla_alpha_b[l], gla_norm_w[l])
        u_h = y_h @ w_branch[l, 0]
        u_g = y_g @ w_branch[l, 1]
        mg = jax.nn.sigmoid(mg.reshape(B, T, N_BRANCH, D_MODEL))
        merged = mg[:, :, 0] * u_h + mg[:, :, 1] * u_g
        x = x + gate[:, None, :] * (merged @ w_out[l])
    return rms_norm(x, final_norm_w)
```

```python
import numpy as np
import concourse.bass as bass
import concourse.mybir as mybir
from concourse.bass_utils import run_bass_kernel_spmd

F32 = mybir.dt.float32
BF16 = mybir.dt.bfloat16
AF = mybir.ActivationFunctionType
ALU = mybir.AluOpType
AX = mybir.AxisListType

D = 1024
NTOK = 2048
TT = 512
NTILE = NTOK // TT
C = 64
NCH = TT // C
EPS = 1e-6
F_FLOOR = 1e-20
NBLK = 24
MG0 = 7184
BLK_COL0 = [512 * i for i in range(14)] + [MG0 + 512 * i for i in range(4)]
SW = 2048 + 12
V_C = 0
V_ADAB = 8
V_NW = 32
V_LB0 = 40
V_LB1 = 48
V_HNW = 56
V_ALB = 64
V_GNW = 68
V_FNW = 70
V_M = 78
V_OM = 82
NV = 86

ENGS = ("pe", "act", "dve", "pool", "sp")


class Sched:
    def __init__(self):
        self.ops = {e: [] for e in ENGS}
        self.cnt = {e: 0 for e in ENGS}
        self.lastw = {}
        self.readers = {}
        self.seen = {e: {} for e in ENGS}
        self.dma_rr = {e: 0 for e in ENGS}
        self.dma_val = {}
        self.ndsem = 6
        self.capture = None
        self.atom_depth = 0
        self.atom_open = False

    def atom_begin(self):
        self.atom_depth += 1
        self.atom_open = False

    def atom_end(self):
        self.atom_depth -= 1
        self.atom_open = False

    def replay_merged(self, A, B):
        na, nb = len(A), len(B)
        ia = ib = 0
        while ia < na or ib < nb:
            take_a = ib >= nb or (ia < na and ia * max(nb, 1) <= ib * max(na, 1))
            atom = A[ia] if take_a else B[ib]
            if take_a:
                ia += 1
            else:
                ib += 1
            for rec in atom:
                self.op(*rec)

    def op(self, eng, fn, reads=(), writes=(), dma=False, cc=False):
        if self.capture is not None:
            rec = (eng, fn, tuple(reads), tuple(writes), dma, cc)
            if self.atom_depth > 0 and self.capture and self.atom_open:
                self.capture[-1].append(rec)
            else:
                self.capture.append([rec])
                self.atom_open = self.atom_depth > 0
            return None
        waits = set()
        for r in reads:
            if r in self.lastw:
                waits.add(self.lastw[r])
        for w in writes:
            if w in self.lastw:
                waits.add(self.lastw[w])
            for ev in self.readers.get(w, ()):
                waits.add(ev)
        if cc:
            key = ("cc",)
            prev = self.dma_val.get(key, 0)
            ev = (key, prev + 1)
            self.dma_val[key] = prev + 1
        elif dma:
            key = ("d", eng, self.dma_rr[eng] % self.ndsem)
            self.dma_rr[eng] += 1
            prev = self.dma_val.get(key, 0)
            if prev:
                waits.add((key, prev))
            ev = (key, prev + 16)
            self.dma_val[key] = prev + 16
        else:
            self.cnt[eng] += 1
            ev = (eng, self.cnt[eng])
        need = {}
        for k, v in waits:
            need[k] = max(need.get(k, 0), v)
        wl = []
        for k, v in need.items():
            if k == "pe" and eng == "pe":
                continue
            if self.seen[eng].get(k, 0) >= v:
                continue
            self.seen[eng][k] = v
            wl.append((k, v))
        self.ops[eng].append((wl, fn, ev))
        for w in writes:
            self.lastw[w] = ev
            self.readers[w] = []
        for r in reads:
            self.readers.setdefault(r, []).append(ev)
        return ev

    def final_wait(self, eng, events):
        self.ops[eng].append(([e for e in events], None, None))


def build_program(phases):
    nc = bass.Bass("TRN2", target_bir_lowering=False)

    xT_d = nc.dram_tensor("xT", [D, NTOK], F32, kind="ExternalInput").ap()
    wblk_d = nc.dram_tensor("wblk", [2, NBLK, 128, 4096], F32, kind="ExternalInput").ap()
    wga_d = nc.dram_tensor("wga", [2, 128, 128], F32, kind="ExternalInput").ap()
    adaw_d = nc.dram_tensor("adaw", [2, 12, 128, 2048], F32, kind="ExternalInput").ap()
    vecs_d = nc.dram_tensor("vecs", [2, 128, NV], F32, kind="ExternalInput").ap()
    alw_d = nc.dram_tensor("alw", [2, 16, 512], F32, kind="ExternalInput").ap()
    ident_d = nc.dram_tensor("ident", [128, 128], F32, kind="ExternalInput").ap()
    ones_d = nc.dram_tensor("ones", [128, 128], F32, kind="ExternalInput").ap()
    causal_d = nc.dram_tensor("causal", [64, 64], F32, kind="ExternalInput").ap()
    rmask_d = nc.dram_tensor("rmask", [128, TT], F32, kind="ExternalInput").ap()
    xo_d = nc.dram_tensor("xo", [D, NTOK], F32, kind="ExternalOutput").ap()
    x1_d = nc.dram_tensor("x1s", [D, NTOK], F32).ap()
    cin_t = [nc.dram_tensor(f"cin{i}", [128, 1036], F32) for i in range(2)]
    cout_t = [nc.dram_tensor(f"cout{i}", [4 * 128, 1036], F32) for i in range(2)]
    cin_d = [t_.ap() for t_ in cin_t]
    cout_d = [t_.ap() for t_ in cout_t]

    S = Sched()
    import contextlib
    es = contextlib.ExitStack()

    def sb(name, shape, dt):
        return es.enter_context(nc.sbuf_tensor(name, shape, dt))

    def ps(name, shape, dt):
        return es.enter_context(nc.psum_tensor(name, shape, dt))

    with es:
        xts = [sb(f"xt{i}", [128, 8 * TT], F32) for i in range(2)]
        tile_n = [0]
        hTs = [sb(f"hT{i}", [128, 8 * TT], BF16) for i in range(2)]
        hcur = [None, None]
        NW = 4
        wbuf = [sb(f"wbuf{i}", [128, 4096], BF16) for i in range(NW)]
        wga = sb("wga_s", [128, 128], BF16)
        alw = sb("alw_s", [16, 512], F32)
        vecs = sb("vecs_s", [128, NV], F32)
        ident = sb("ident_s", [128, 128], BF16)
        ones = sb("ones_s", [128, 128], BF16)
        causal = sb("causal_s", [64, 64], F32)
        rmask = sb("rmask_s", [128, TT], BF16)
        mod = sb("mod", [128, 24], F32)
        cact = sb("cact", [128, 8], F32)
        nw1 = sb("nw1", [128, 8], F32)
        lb = sb("lb", [128, 8], F32)
        oml = sb("oml", [128, 8], F32)
        nalb = sb("nalb", [128, 4], F32)
        noml = sb("noml", [128, 8], F32)
        lbe = sb("lbe", [128, 8], F32)
        NSET = 2
        Tsets = [[sb(f"T{i}_{p}", [128, TT], F32) for i in range(6)] for p in range(NSET)]
        qts = [sb(f"qt_{p}", [128, TT], BF16) for p in range(NSET)]
        kts = [sb(f"kt_{p}", [128, TT], BF16) for p in range(NSET)]
        khs = [sb(f"kh_{p}", [128, TT], BF16) for p in range(NSET)]
        qes = [sb(f"qe_{p}", [128, TT], BF16) for p in range(NSET)]
        sms = [sb(f"sm_{p}", [128, 4 * NCH], F32) for p in range(NSET)]
        T = Tsets[0]
        sqb = [sb(f"sqb{i}", [128, TT], BF16) for i in range(2)]
        tmpx = [sb(f"tmpx{i}", [128, TT], F32) for i in range(2)]
        rstd = sb("rstd", [128, TT], F32)
        v_tms = [sb(f"v_tm{i}", [64, NCH * 512], BF16) for i in range(2)]
        mv_n = [0]
        khtms = [sb(f"khtm_{p}", [64, NCH * 128], BF16) for p in range(NSET)]
        state = sb("state", [128, 2048], F32)
        bsum = sb("bsum", [128, 12], F32)
        btmp = sb("btmp", [128, 1], F32)
        ga_sb = sb("ga_sb", [16, TT], F32)
        A_sbs = [sb(f"A_sb_{p}", [64, NCH * 64], BF16) for p in range(NSET)]
        Salls = [sb(f"Sall_{p}", [128, NCH * 256], F32) for p in range(NSET)]
        Sts = [sb(f"St_{p}", [128, NCH * 256], BF16) for p in range(NSET)]
        hs_n = [0]
        oy = sb("oy", [128, 16 * TT], BF16)
        ssacc = sb("ssacc", [128, TT], F32)
        merged = sb("merged", [128, 8 * TT], BF16)
        sdec = sb("sdec", [128, 12], F32)
        dm = sb("dm", [128, 12], F32)
        sfin = sb("sfin", [128, 12], F32)
        cst = sb("cst", [128, 3], F32)
        pp = [ps(f"pp{i}", [128, 512], F32) for i in range(3)]
        Pp = [ps(f"Pp{i}", [128, 512], F32) for i in range(2)]
        TR = ps("TR", [128, 1024], BF16)
        AT = ps("AT", [128, 512], F32)
        OT = ps("OT", [128, 512], F32)

        out_events = []
        setup_layer = [None]
        preload = {}

        def phase(mode, layer, last, prefetch_next=False):
            full = mode == "B"
            st = {"pp": 0, "w": 0}

            def v3(t, inner):
                return t[:].rearrange("p (c t) -> p c t", t=inner)

            def dma(eng, out, in_, reads, writes):
                return S.op(eng, lambda e: e.dma_start(out=out, in_=in_), reads, writes, dma=True)

            def act(out, in_, func, reads, writes, **kw):
                return S.op("act", lambda e: e.activation(out=out, in_=in_, func=func, **kw), reads, writes)

            def tt(out, in0, in1, op, reads, writes, eng="dve"):
                return S.op(eng, lambda e: e.tensor_tensor(out=out, in0=in0, in1=in1, op=op), reads, writes)

            def ts(out, in0, s1, s2, op0, op1, reads, writes, eng="dve"):
                if op1 is None:
                    return S.op(eng, lambda e: e.tensor_scalar(out=out, in0=in0, scalar1=s1, scalar2=None, op0=op0), reads, writes)
                return S.op(eng, lambda e: e.tensor_scalar(out=out, in0=in0, scalar1=s1, scalar2=s2, op0=op0, op1=op1), reads, writes)

            def stt(out, in0, scalar, in1, op0, op1, reads, writes):
                return S.op("dve", lambda e: e.scalar_tensor_tensor(out=out, in0=in0, scalar=scalar, in1=in1, op0=op0, op1=op1), reads, writes)

            def mm(out, lhsT, rhs, start, stop, reads, writes):
                return S.op("pe", lambda e: e.matmul(out, lhsT=lhsT, rhs=rhs, start=start, stop=stop), reads, writes)

            def next_pp():
                i = st["pp"] % len(pp)
                st["pp"] += 1
                return i

            wslot = {}

            def load_w(blk, i):
                wslot[blk] = i
                if preload.get(blk) == i:
                    del preload[blk]
                    return i
                dma("pool", wbuf[i][:], wblk_d[layer, blk], [], [f"w{i}"])
                return i

            def w3(blk):
                return wbuf[wslot[blk]][:].rearrange("p (k c) -> p k c", c=512), f"w{wslot[blk]}"

            hTs3 = [h_[:].rearrange("p (k t) -> p k t", t=TT) for h_ in hTs]
            prepped = {}
            xts3 = [x_[:].rearrange("p (k t) -> p k t", t=TT) for x_ in xts]
            loaded = {}
            tile_base = tile_n[0]
            tile_n[0] += NTILE

            def proj_fm(blk, g):
                w, wr = w3(blk)
                i = next_pp()
                for kc in range(8):
                    mm(pp[i][:], w[:, kc, g * 128:(g + 1) * 128], hcur[0][:, kc, :], kc == 0, kc == 7, [wr, hcur[1]], [f"pp{i}"])
                return i

            if setup_layer[0] != layer:
                dma("sp", vecs[:], vecs_d[layer], [], ["vecs"])
                dma("sp", alw[:], alw_d[layer], [], ["alw"])
                dma("pool", wga[:], wga_d[layer], [], ["wga"])
            need_setup = setup_layer[0] != layer
            setup_layer[0] = layer
            if need_setup:
                act(cact[:], vecs[:, V_C:V_C + 8], AF.Silu, ["vecs"], ["cact"])
            mi = next_pp()
            ada_slots = [(hTs[0][:].bitcast(F32), ["hT0"]), (hTs[1][:].bitcast(F32), ["hT1"]),
                         (oy[:, 0:4096].bitcast(F32), [f"oy{g_}" for g_ in range(8)]),
                         (oy[:, 4096:8192].bitcast(F32), [f"oy{g_}" for g_ in range(8, 16)])]
            for blk in range(12 if need_setup else 0):
                adab_ap, ares = ada_slots[blk % 4]
                ada3 = adab_ap.rearrange("p (k c) -> p k c", c=256)
                dma("sp", adab_ap, adaw_d[layer, blk], [], ares)
                for g in range(2):
                    j = blk * 2 + g
                    for kc in range(8):
                        mm(pp[mi][:, j:j + 1], ada3[:, kc, g * 128:(g + 1) * 128], cact[:, kc:kc + 1], kc == 0, kc == 7,
                           ares + ["cact"], [f"pp{mi}"])
            if need_setup:
                tt(mod[:, 0:24], pp[mi][:, 0:24], vecs[:, V_ADAB:V_ADAB + 24], ALU.add, [f"pp{mi}", "vecs"], ["mod"])
                stt(nw1[:], mod[:, 8:16], 1.0, vecs[:, V_NW:V_NW + 8], ALU.add, ALU.mult, ["mod", "vecs"], ["nw1"])
            if not need_setup:
                pass
            elif layer == 0:
                S.op("dve", lambda e: e.memset(lb[:], 0.0), [], ["lb"])
                S.op("dve", lambda e: e.memset(oml[:], 1.0), [], ["oml"])
            else:
                tt(lb[:], vecs[:, V_LB1:V_LB1 + 8], vecs[:, V_LB0:V_LB0 + 8], ALU.subtract, ["vecs"], ["lb"])
                act(lb[:], lb[:], AF.Sigmoid, ["lb"], ["lb"])
                ts(oml[:], lb[:], -1.0, 1.0, ALU.mult, ALU.add, ["lb"], ["oml"])
            if need_setup:
                ts(nalb[:], vecs[:, V_ALB:V_ALB + 4], -1.0, None, ALU.mult, None, ["vecs"], ["nalb"])
                ts(noml[:], oml[:], -1.0, None, ALU.mult, None, ["oml"], ["noml"])
                ts(lbe[:], lb[:], F_FLOOR, None, ALU.add, None, ["lb"], ["lbe"])
            S.op("dve", lambda e: e.memset(bsum[:], 0.0), [], ["bsum"])
            init_done = [False, False]

            def init_half(i):
                if init_done[i]:
                    return
                init_done[i] = True
                c_lo, c_hi = i * 1024, (i + 1) * 1024
                d0, nd = (0, 8) if i == 0 else (8, 4)
                S.op("dve", lambda e: e.memset(state[:, c_lo:c_hi], 0.0), [], ["state"])
                if full:
                    sblk = merged[:].bitcast(F32)
                    MG = [f"mg{d_}" for d_ in range(4 * i, 4 * i + 4)]
                    for jp in range(4):
                        dma("sp", sblk[:, c_lo:c_hi], cout_d[i][jp * 128:(jp + 1) * 128, 0:1024], [f"cout{i}"], MG)
                        dma("sp", sdec[:, d0:d0 + nd], cout_d[i][jp * 128:(jp + 1) * 128, 1024:1024 + nd], [f"cout{i}"], [f"sdec{i}"])
                        ts(dm[:, d0:d0 + nd], sdec[:, d0:d0 + nd], vecs[:, V_M + jp:V_M + jp + 1], vecs[:, V_OM + jp:V_OM + jp + 1],
                           ALU.mult, ALU.add, [f"sdec{i}", "vecs"], [f"dm{i}"])
                        for h in range(d0, d0 + nd):
                            c0, c1 = (h * 128, (h + 1) * 128) if h < 8 else (1024 + (h - 8) * 256, 1024 + (h - 7) * 256)
                            ts(state[:, c0:c1], state[:, c0:c1], dm[:, h:h + 1], None, ALU.mult, None, ["state", f"dm{i}"], ["state"])
                        stt(state[:, c_lo:c_hi], sblk[:, c_lo:c_hi], vecs[:, V_M + jp:V_M + jp + 1], state[:, c_lo:c_hi], ALU.mult, ALU.add,
                            MG + ["vecs", "state"], ["state"])

            def exchange(i):
                d0, nd = (0, 8) if i == 0 else (8, 4)
                act(sfin[:, d0:d0 + nd], bsum[:, d0:d0 + nd], AF.Exp, ["bsum"], [f"sfin{i}"], scale=(1.0 if i == 0 else -1.0 / 16.0))
                dma("pool", cin_d[i][:, 0:1024], state[:, i * 1024:(i + 1) * 1024], ["state", f"cout{i}"], [f"cin{i}"])
                dma("pool", cin_d[i][:, 1024:1024 + nd], sfin[:, d0:d0 + nd], [f"sfin{i}"], [f"cin{i}"])
                S.op("pool", lambda e: e.collective_compute("AllGather", ALU.bypass, replica_groups=[[0, 1, 2, 3], [4, 5, 6, 7]],
                                                            ins=[cin_t[i].ap().opt()], outs=[cout_t[i].ap().opt()]),
                     [f"cin{i}"], [f"cout{i}"], cc=True)

            def rstd_from(psrc, reads, inv_n):
                act(rstd[:], psrc, AF.Ln, list(reads) + ["cst"], ["rstd"], scale=inv_n, bias=vecs_eps)
                act(rstd[:], rstd[:], AF.Exp, ["rstd"], ["rstd"], scale=-0.5)

            def sumsq_rows(srcs, reads_list, inv_n):
                i = next_pp()
                n = len(srcs)
                for k, (src, rd) in enumerate(zip(srcs, reads_list)):
                    b = k % 2
                    act(sqb[b][:], src, AF.Square, rd, [f"sqb{b}"])
                    mm(pp[i][:], ones[:], sqb[b][:], k == 0, k == n - 1, ["ones", f"sqb{b}"], [f"pp{i}"])
                rstd_from(pp[i][:], [f"pp{i}"], inv_n)

            def load_x(t_):
                if t_ in loaded or t_ >= NTILE:
                    return
                p_ = (tile_base + t_) % 2
                loaded[t_] = p_
                dma("sp", xts3[p_], (xT_d if layer == 0 else x1_d).rearrange("(k p) t -> p k t", p=128)[:, :, t_ * TT:(t_ + 1) * TT],
                    [] if layer == 0 else [f"x1_{t_}"], [f"xt{p_}"])

            def prep_h(t_):
                if t_ in prepped or t_ >= NTILE:
                    return
                prepped[t_] = True
                p_ = (tile_base + t_) % 2
                x3, XR_ = xts3[p_], f"xt{p_}"
                load_x(t_)
                sumsq_rows([x3[:, kc, :] for kc in range(8)], [[XR_]] * 8, 1.0 / D)
                for kc in range(8):
                    b = kc % 2
                    tt(tmpx[b][:], x3[:, kc, :], rstd[:], ALU.mult, [XR_, "rstd"], [f"tmpx{b}"])
                    act(hTs3[p_][:, kc, :], tmpx[b][:], AF.Identity, [f"tmpx{b}", "nw1", "mod"], [f"hT{p_}"],
                        scale=nw1[:, kc:kc + 1], bias=mod[:, kc:kc + 1])

            def make_hT(t):
                xp = (tile_base + t) % 2
                prep_h(t)
                hcur[0], hcur[1] = hTs3[xp], f"hT{xp}"
                load_x(t + 1)
                return xts3[xp], f"xt{xp}"

            def make_v(blk):
                w, wr = w3(blk)
                vi = mv_n[0] % 2
                mv_n[0] += 1
                vt3 = v_tms[vi][:].rearrange("p (c n) -> p c n", n=512)
                for tsub in range(TT // 128):
                    i = next_pp()
                    for kc in range(8):
                        mm(pp[i][:], hcur[0][:, kc, tsub * 128:(tsub + 1) * 128], w[:, kc, :], kc == 0, kc == 7, [wr, hcur[1]], [f"pp{i}"])
                    act(vt3[:, 2 * tsub, :], pp[i][0:64, :], AF.Copy, [f"pp{i}"], [f"v_tm{vi}"])
                    S.op("dve", lambda e, o=vt3[:, 2 * tsub + 1, :], a=pp[i][64:128, :]: e.tensor_copy(out=o, in_=a), [f"pp{i}"], [f"v_tm{vi}"])
                return vi

            def head_scan(hidx, V, vcol0, vi, sc, cum_src_fn):
                scol0 = hidx * 128 if hidx < 8 else 1024 + (hidx - 8) * 256
                p = hs_n[0] % NSET
                hs_n[0] += 1
                T, qt, kt, kh, qe, sm = Tsets[p], qts[p], kts[p], khs[p], qes[p], sms[p]
                A_sb, khtm, Sall, St = A_sbs[p], khtms[p], Salls[p], Sts[p]

                def N(nm):
                    return f"{nm}_{p}"
                cum_src_fn(T, N)
                S.op("dve", lambda e: e.tensor_tensor_scan(out=T[4][:], data0=rmask[:], data1=T[3][:], initial=0.0,
                                                            op0=ALU.mult, op1=ALU.add), ["rmask", N("T3")], [N("T4")])
                b3 = v3(T[4], C)
                sm3 = sm[:].rearrange("p (a c) -> p a c", c=NCH)
                act(sm3[:, 3, :].unsqueeze(2), b3[:, :, C - 1:C], AF.Exp, [N("T4")], [N("El")], scale=sc)
                S.op("dve", lambda e: e.tensor_reduce(out=btmp[:], in_=b3[:, :, C - 1:C],
                                                      axis=AX.XY, op=ALU.add), [N("T4")], ["btmp"])
                tt(bsum[:, hidx:hidx + 1], bsum[:, hidx:hidx + 1], btmp[:], ALU.add, ["btmp", "bsum"], ["bsum"])
                if not full:
                    tt(v3(T[5], C), b3, b3[:, :, C - 1:C].broadcast_to([128, NCH, C]), ALU.subtract, [N("T4")], [N("T5")])
                    act(T[3][:], T[5][:], AF.Exp, [N("T5")], [N("T3")], scale=-sc)
                    tt(kh[:], T[2][:], T[3][:], ALU.mult, [N("T2"), N("T3")], [N("kh")])
                    yield
                else:
                    ref = C // 2 - 1
                    tt(v3(T[5], C), b3, b3[:, :, ref:ref + 1].broadcast_to([128, NCH, C]), ALU.subtract, [N("T4")], [N("T5")])
                    act(T[1][:], T[5][:], AF.Exp, [N("T5")], [N("T1")], scale=-sc)
                    tt(sm3[:, 0, :].unsqueeze(2), b3[:, :, C - 1:C], b3[:, :, ref:ref + 1], ALU.subtract, [N("T4")], [N("blm")])
                    act(sm3[:, 1, :], sm3[:, 0, :], AF.Exp, [N("blm")], [N("Dk")], scale=sc)
                    tt(kt[:], T[2][:], T[1][:], ALU.mult, [N("T2"), N("T1")], [N("kt")])
                    tt(v3(kh, C), v3(kt, C), sm3[:, 1, :].unsqueeze(2).broadcast_to([128, NCH, C]), ALU.mult, [N("kt"), N("Dk")], [N("kh")])
                    act(T[3][:], T[5][:], AF.Exp, [N("T5")], [N("T3")], scale=sc)
                    tt(qt[:], T[0][:], T[3][:], ALU.mult, [N("T0"), N("T3")], [N("qt")])
                    act(T[4][:], T[4][:], AF.Exp, [N("T4")], [N("T4")], scale=sc)
                    tt(qe[:], T[0][:], T[4][:], ALU.mult, [N("T0"), N("T4")], [N("qe")])
                    yield
                TR3 = TR[:].rearrange("p (c k) -> p c k", k=128)
                for c in range(NCH):
                    S.op("pe", lambda e, c=c: e.transpose(out=TR3[0:C, c, :], in_=kh[:, c * C:(c + 1) * C], identity=ident[:]),
                         [N("kh"), "ident"], ["TR"])
                act(khtm[:], TR[0:C, :], AF.Copy, ["TR"], [N("khtm")])
                if full:
                    AT3 = AT[:].rearrange("p (c t) -> p c t", t=C)
                    h2 = C // 2
                    for c in range(NCH):
                        mm(AT3[0:C, c, h2:C], kt[:, c * C:(c + 1) * C], qt[:, c * C + h2:(c + 1) * C], True, True, [N("kt"), N("qt")], ["AT"])
                        mm(AT3[0:h2, c, 0:h2], kt[:, c * C:c * C + h2], qt[:, c * C:c * C + h2], True, True, [N("kt"), N("qt")], ["AT"])
                    tt(v3(A_sb, C), AT3[0:C, :, :], causal[:].unsqueeze(1).broadcast_to([C, NCH, C]), ALU.mult, ["AT", "causal"], [N("A_sb")])
                kh3 = khtm[:].rearrange("p (c k) -> p c k", k=128)
                vt3 = v_tms[vi][:].rearrange("p (c n) -> p c n", n=512)
                per = 512 // V
                if full:
                    Sall3 = Sall[:].rearrange("p (c v) -> p c v", v=256)
                    St3 = St[:].rearrange("p (c v) -> p c v", v=256)

                def S_at(c):
                    if c == 0 or c == NCH:
                        return state[:, scol0:scol0 + V], "state"
                    return Sall3[:, c, 0:V], N(f"Sall{c}")

                for c in range(NCH):
                    pi = (c // per) % 2
                    po = (c % per) * V
                    if c % (2 * per) == 0:
                        for c2 in range(c, min(c + 2 * per, NCH)):
                            pi2 = (c2 // per) % 2
                            po2 = (c2 % per) * V
                            mm(Pp[pi2][:, po2:po2 + V], kh3[:, c2, :], vt3[:, c2, vcol0:vcol0 + V], True, True,
                               [N("khtm"), f"v_tm{vi}"], [f"Pp{pi2}"])
                    if full:
                        src, sr = S_at(c)
                        act(St3[:, c, 0:V], src, AF.Copy, [sr], [N(f"St{c}")])
                        dst, dr = S_at(c + 1)
                        stt(dst, src, sm3[:, 3, c:c + 1], Pp[pi][:, po:po + V], ALU.mult, ALU.add,
                            [sr, N("El"), f"Pp{pi}"], [dr])
                    else:
                        stt(state[:, scol0:scol0 + V], state[:, scol0:scol0 + V], sm3[:, 3, c:c + 1], Pp[pi][:, po:po + V],
                            ALU.mult, ALU.add, ["state", N("El"), f"Pp{pi}"], ["state"])
                if full:
                    outs = []
                    for vg in range(V // 128):
                        for c in range(NCH):
                            mm(OT[:, c * C:(c + 1) * C], vt3[:, c, vcol0 + vg * 128:vcol0 + (vg + 1) * 128], v3(A_sb, C)[:, c, :],
                               True, False, [f"v_tm{vi}", N("A_sb")], ["OT"])
                            mm(OT[:, c * C:(c + 1) * C], St3[:, c, vg * 128:(vg + 1) * 128], qe[:, c * C:(c + 1) * C],
                               False, True, [N(f"St{c}"), N("qe")], ["OT"])
                        gi = (hidx if hidx < 8 else 8 + (hidx - 8) * 2 + vg)
                        o = oy[:, gi * TT:(gi + 1) * TT]
                        b = gi % 2
                        act(sqb[b][:], OT[:], AF.Square, ["OT"], [f"sqb{b}", "OT_rd"])
                        S.op("dve", lambda e, o=o: e.tensor_copy(out=o, in_=OT[:]), ["OT"], [f"oy{gi}", "OT_rd"])
                        outs.append((gi, b))
                    return outs
                return None

            def hgrn_src(blk_f, g, hh, blk_q):
                def fn(T, N):
                    if full:
                        S.atom_begin()
                        iq = proj_fm(blk_q, g)
                        act(T[0][:], pp[iq][:], AF.Exp, [f"pp{iq}"], [N("T0")], scale=-1.0)
                        act(T[0][:], T[0][:], AF.Ln, [N("T0"), "cst"], [N("T0")], bias=vecs_one)
                        act(T[0][:], T[0][:], AF.Exp, [N("T0")], [N("T0")], scale=-1.0)
                        tt(T[0][:], T[0][:], pp[iq][:], ALU.mult, [N("T0"), f"pp{iq}"], [N("T0")])
                        S.atom_end()
                    S.atom_begin()
                    i = proj_fm(blk_f, g)
                    act(T[1][:], pp[i][:], AF.Exp, [f"pp{i}"], [N("T1")], scale=-1.0)
                    S.atom_end()
                    act(T[1][:], T[1][:], AF.Ln, [N("T1"), "cst"], [N("T1")], bias=vecs_one)
                    act(T[1][:], T[1][:], AF.Exp, [N("T1")], [N("T1")], scale=-1.0)
                    ts(T[2][:], T[1][:], noml[:, hh:hh + 1], oml[:, hh:hh + 1], ALU.mult, ALU.add, [N("T1"), "oml", "noml"], [N("T2")])
                    act(T[3][:], T[1][:], AF.Ln, [N("T1"), "oml", "lbe"], [N("T3")], scale=oml[:, hh:hh + 1], bias=lbe[:, hh:hh + 1])
                return fn

            def gla_src(gh):
                def fn(T, N):
                    if full:
                        S.atom_begin()
                        iq = proj_fm(8, gh)
                        act(T[0][:], pp[iq][:], AF.Identity, [f"pp{iq}"], [N("T0")], scale=128.0 ** -0.5)
                        S.atom_end()
                    S.atom_begin()
                    ik = proj_fm(9, gh)
                    act(T[2][:], pp[ik][:], AF.Copy, [f"pp{ik}"], [N("T2")])
                    S.atom_end()
                    S.atom_begin()
                    iz = next_pp()
                    mm(pp[iz][:], alw[:, gh * 128:(gh + 1) * 128], ga_sb[:], True, True, ["alw", "ga_sb"], [f"pp{iz}"])
                    act(T[1][:], pp[iz][:], AF.Exp, [f"pp{iz}", "nalb"], [N("T1")], scale=-1.0, bias=nalb[:, gh:gh + 1])
                    S.atom_end()
                    act(T[3][:], T[1][:], AF.Ln, [N("T1"), "cst"], [N("T3")], bias=vecs_one)
                return fn

            vecs_eps = cst[:, 0:1]
            vecs_one = cst[:, 1:2]

            for t in range(NTILE):
                xt3, XR = make_hT(t)
                def fin(gen):
                    try:
                        next(gen)
                    except StopIteration as e_:
                        return e_.value
                    raise RuntimeError("head_scan did not finish")

                def fin_hgrn(pend):
                    gen, hh = pend
                    init_half(0)
                    res = fin(gen)
                    if full:
                        (gi, b), = res
                        S.atom_begin()
                        i = next_pp()
                        mm(pp[i][:], ones[:], sqb[b][:], True, True, ["ones", f"sqb{b}"], [f"pp{i}"])
                        if hh == 0:
                            S.op("dve", lambda e, i=i: e.tensor_copy(out=ssacc[:], in_=pp[i][:]), [f"pp{i}"], ["ssacc"])
                        else:
                            tt(ssacc[:], ssacc[:], pp[i][:], ALU.add, [f"pp{i}", "ssacc"], ["ssacc"])
                        S.atom_end()

                def pipelined(gen, fin_fn, pend):
                    S.capture = []
                    next(gen)
                    A = S.capture
                    S.capture = []
                    if pend is not None:
                        fin_fn(pend)
                    B = S.capture
                    S.capture = None
                    S.replay_merged(A, B)

                pend = None
                load_w(4, 2)
                load_w(5, 3)
                if full:
                    load_w(0, 0)
                load_w(2, 1)
                vis = [make_v(4), make_v(5)]
                if full:
                    load_w(1, 2)
                load_w(3, 3)
                for qd in range(2):
                    vi = vis[qd]
                    for g in range(4):
                        hh = qd * 4 + g
                        gen = head_scan(hh, 128, g * 128, vi, 1.0, hgrn_src(2 + qd, g, hh, qd))
                        pipelined(gen, fin_hgrn, pend)
                        pend = (gen, hh)
                fin_hgrn(pend)
                if full:
                    rstd_from(ssacc[:], ["ssacc"], 1.0 / 1024)
                    for g in range(8):
                        if g % 4 == 0:
                            load_w(6 + g // 4, 0 if g == 0 else 1)
                        i = proj_fm(6 + g // 4, g % 4)
                        act(T[0][:], pp[i][:], AF.Silu, [f"pp{i}"], ["T0_0"])
                        o = oy[:, g * TT:(g + 1) * TT]
                        tt(T[1][:], o, rstd[:], ALU.mult, [f"oy{g}", "rstd"], ["T1_0"])
                        stt(o, T[1][:], vecs[:, V_HNW + g:V_HNW + g + 1], T[0][:], ALU.mult, ALU.mult, ["T1_0", "T0_0", "vecs"], [f"oy{g}"])
                if not full:
                    prep_h(t + 1)
                if full:
                    load_w(8, 2)
                load_w(9, 3)
                load_w(10, 0)
                load_w(11, 1)
                if not full and t == NTILE - 1:
                    exchange(0)
                i = next_pp()
                for kc in range(8):
                    mm(pp[i][0:16, :], wga[:, kc * 16:(kc + 1) * 16], hcur[0][:, kc, :], kc == 0, kc == 7, ["wga", hcur[1]], [f"pp{i}"])
                act(ga_sb[:], pp[i][0:16, :], AF.Copy, [f"pp{i}"], ["ga_sb"])
                def fin_gla(pend):
                    gen, gh = pend
                    init_half(1)
                    res = fin(gen)
                    if full:
                        S.atom_begin()
                        i = next_pp()
                        for k, (gi, b) in enumerate(res):
                            mm(pp[i][:], ones[:], sqb[b][:], k == 0, k == 1, ["ones", f"sqb{b}"], [f"pp{i}"])
                        rstd_from(pp[i][:], [f"pp{i}"], 1.0 / 256)
                        S.atom_end()
                        for vg, (gi, b) in enumerate(res):
                            S.atom_begin()
                            iz = proj_fm(12 + gh // 2, (gh % 2) * 2 + vg)
                            act(tmpx[0][:], pp[iz][:], AF.Silu, [f"pp{iz}"], ["tmpx0"])
                            S.atom_end()
                            o = oy[:, gi * TT:(gi + 1) * TT]
                            tt(tmpx[1][:], o, rstd[:], ALU.mult, [f"oy{gi}", "rstd"], ["tmpx1"])
                            stt(o, tmpx[1][:], vecs[:, V_GNW + vg:V_GNW + vg + 1], tmpx[0][:], ALU.mult, ALU.mult, ["tmpx1", "tmpx0", "vecs"], [f"oy{gi}"])

                pend = None
                gvis = [make_v(10), make_v(11)]
                if full:
                    load_w(12, 0)
                    load_w(13, 1)
                for gh in range(4):
                    vi = gvis[gh // 2]
                    gen = head_scan(8 + gh, 256, (gh % 2) * 256, vi, -1.0 / 16.0, gla_src(gh))
                    pipelined(gen, fin_gla, pend)
                    pend = (gen, gh)
                fin_gla(pend)
                if not full:
                    continue
                mg3 = merged[:].rearrange("p (k t) -> p k t", t=TT)
                for half in range(2):
                    load_w(14 + half, 2)
                    load_w(16 + half, 3)
                    load_w(18 + half, 0)
                    load_w(20 + half, 1)
                    for g in range(4):
                        dg = half * 4 + g
                        i0 = proj_fm(14 + half, g)
                        act(T[0][:], pp[i0][:], AF.Sigmoid, [f"pp{i0}"], ["T0_0"])
                        i1 = proj_fm(16 + half, g)
                        act(T[1][:], pp[i1][:], AF.Sigmoid, [f"pp{i1}"], ["T1_0"])
                        for br, tgt, gate_t in ((0, 2, 0), (1, 3, 1)):
                            w, wr = w3(18 + 2 * br + half)
                            iu = next_pp()
                            for k in range(8):
                                mm(pp[iu][:], w[:, k, g * 128:(g + 1) * 128], oy[:, (br * 8 + k) * TT:(br * 8 + k + 1) * TT],
                                   k == 0, k == 7, [wr, f"oy{br * 8 + k}"], [f"pp{iu}"])
                            tt(T[tgt][:], pp[iu][:], T[gate_t][:], ALU.mult, [f"pp{iu}", f"T{gate_t}_0"], [f"T{tgt}_0"])
                        tt(mg3[:, dg, :], T[2][:], T[3][:], ALU.add, ["T2_0", "T3_0"], [f"mg{dg}"])
                prep_h(t + 1)
                for half in range(2):
                    load_w(22 + half, half)
                    w, wr = w3(22 + half)
                    for g in range(4):
                        dg = half * 4 + g
                        io = next_pp()
                        for k in range(8):
                            mm(pp[io][:], w[:, k, g * 128:(g + 1) * 128], mg3[:, k, :], k == 0, k == 7, [wr, f"mg{k}"], [f"pp{io}"])
                        stt(xt3[:, dg, :], pp[io][:], mod[:, 16 + dg:17 + dg], xt3[:, dg, :], ALU.mult, ALU.add,
                            [f"pp{io}", "mod", XR], [XR])
                if last:
                    sumsq_rows([xt3[:, kc, :] for kc in range(8)], [[XR]] * 8, 1.0 / D)
                    for kc in range(8):
                        tt(tmpx[kc % 2][:], xt3[:, kc, :], rstd[:], ALU.mult, [XR, "rstd"], [f"tmpx{kc % 2}"])
                        ts(xt3[:, kc, :], tmpx[kc % 2][:], vecs[:, V_FNW + kc:V_FNW + kc + 1], None, ALU.mult, None,
                           [f"tmpx{kc % 2}", "vecs"], [XR])
                if last:
                    ev = dma("sp", xo_d.rearrange("(k p) t -> p k t", p=128)[:, :, t * TT:(t + 1) * TT], xt3, [XR], ["xo"])
                    out_events.append(ev)
                else:
                    dma("sp", x1_d.rearrange("(k p) t -> p k t", p=128)[:, :, t * TT:(t + 1) * TT], xt3, [XR], [f"x1_{t}"])

            if not full:
                if prefetch_next:
                    for blk_, sl_ in ((4, 2), (5, 3), (0, 0), (2, 1)):
                        load_w(blk_, sl_)
                        preload[blk_] = sl_
                exchange(1)

        S.op("sp", lambda e: e.dma_start(out=causal[:], in_=causal_d[:, :]), [], ["causal"], dma=True)
        S.op("pool", lambda e: e.dma_start(out=rmask[:], in_=rmask_d[:, :]), [], ["rmask"], dma=True)
        S.op("pool", lambda e: e.dma_start(out=ident[:], in_=ident_d[:, :]), [], ["ident"], dma=True)
        S.op("pool", lambda e: e.dma_start(out=ones[:], in_=ones_d[:, :]), [], ["ones"], dma=True)
        S.op("dve", lambda e: e.memset(cst[:, 0:1], EPS), [], ["cst"])
        S.op("dve", lambda e: e.memset(cst[:, 1:2], 1.0), [], ["cst"])
        S.op("dve", lambda e: e.memset(cst[:, 2:3], 0.0), [], ["cst"])
        S.op("dve", lambda e: e.memset(AT[:], 0.0), [], ["AT"])
        for i, (mode, layer) in enumerate(phases):
            nxt = phases[i + 1] if i + 1 < len(phases) else None
            phase(mode, layer, i == len(phases) - 1, prefetch_next=(mode == "A" and nxt == ("B", layer)))
        S.final_wait("sp", out_events)

        sem = {}
        for e in ENGS:
            sem[e] = es.enter_context(nc.semaphore(f"s_{e}"))
        sem[("cc",)] = es.enter_context(nc.semaphore("s_cc"))
        for e in ("sp", "pool"):
            for i in range(S.ndsem):
                sem[("d", e, i)] = es.enter_context(nc.semaphore(f"d_{e}{i}"))
        block = es.enter_context(nc.Block())

        def emit(name, eng):
            for wl, fn, ev in S.ops[name]:
                for k, v in wl:
                    eng.wait_ge(sem[k], v)
                if fn is None:
                    continue
                inst = fn(eng)
                k, v = ev
                inst.then_inc(sem[k], 16 if (isinstance(k, tuple) and k[0] == "d") else 1)

        @block.tensor
        def _(e):
            emit("pe", e)

        @block.scalar
        def _(e):
            emit("act", e)

        @block.vector
        def _(e):
            emit("dve", e)

        @block.gpsimd
        def _(e):
            emit("pool", e)

        @block.sync
        def _(e):
            emit("sp", e)
    return nc


def _pm(v):
    v = np.asarray(v, np.float32)
    return np.ascontiguousarray(v.reshape(-1, 128).T)


def _blk(w, c0, n=512):
    b = w[:, c0:c0 + n].reshape(8, 128, n).transpose(1, 0, 2)
    return np.ascontiguousarray(b.reshape(128, 8 * n))


_PROG = {}


def _prog():
    if "p" not in _PROG:
        _PROG["p"] = build_program([("A", 0), ("B", 0), ("A", 1), ("B", 1)])
    return _PROG["p"]


def kernel(x, c, ada_w, ada_b, norm_w, w_in, hgrn_lb_logits, hgrn_norm_w, gla_alpha_w, gla_alpha_b, gla_norm_w,
           w_branch, w_out, final_norm_w):
    x = np.asarray(x, np.float32)
    ncore = 8
    consts = {
        "ident": np.eye(128, dtype=np.float32),
        "ones": np.ones((128, 128), np.float32),
        "causal": np.triu(np.ones((64, 64), np.float32)),
        "rmask": np.tile((np.arange(TT) % C != 0).astype(np.float32)[None, :], (128, 1)),
    }
    xTs = [np.ascontiguousarray(x[ci // 4, (ci % 4) * NTOK:(ci % 4 + 1) * NTOK, :].T) for ci in range(ncore)]
    walls, wgas, adaws, alws = [], [], [], []
    for l in range(2):
        wi = np.asarray(w_in[l], np.float32)
        blocks = [_blk(wi, c0) for c0 in BLK_COL0]
        blocks += [_blk(np.asarray(w_branch[l, 0], np.float32), 0), _blk(np.asarray(w_branch[l, 0], np.float32), 512)]
        blocks += [_blk(np.asarray(w_branch[l, 1], np.float32), 0), _blk(np.asarray(w_branch[l, 1], np.float32), 512)]
        blocks += [_blk(np.asarray(w_out[l], np.float32), 0), _blk(np.asarray(w_out[l], np.float32), 512)]
        walls.append(np.stack(blocks))
        wgas.append(_blk(wi, 7168, 16))
        adaws.append(np.stack([_blk(np.asarray(ada_w[l], np.float32), 256 * i, 256) for i in range(12)]))
        alws.append(np.asarray(gla_alpha_w[l], np.float32))
    wall = np.ascontiguousarray(np.stack(walls))
    wga = np.ascontiguousarray(np.stack(wgas))
    adaw = np.ascontiguousarray(np.stack(adaws))
    alw = np.ascontiguousarray(np.stack(alws))
    in_maps = []
    for ci in range(ncore):
        b, j = ci // 4, ci % 4
        v = np.zeros((2, 128, NV), np.float32)
        for l in range(2):
            v[l, :, V_C:V_C + 8] = _pm(c[b])
            v[l, :, V_ADAB:V_ADAB + 24] = _pm(ada_b[l])
            v[l, :, V_NW:V_NW + 8] = _pm(norm_w[l])
            v[l, :, V_LB0:V_LB0 + 8] = _pm(hgrn_lb_logits[0])
            v[l, :, V_LB1:V_LB1 + 8] = _pm(hgrn_lb_logits[1])
            v[l, :, V_HNW:V_HNW + 8] = _pm(hgrn_norm_w[l])
            v[l, :, V_ALB:V_ALB + 4] = _pm(gla_alpha_b[l])
            v[l, :, V_GNW:V_GNW + 2] = _pm(gla_norm_w[l])
            v[l, :, V_FNW:V_FNW + 8] = _pm(final_norm_w)
            for jp in range(4):
                v[l, :, V_M + jp] = 1.0 if jp < j else 0.0
                v[l, :, V_OM + jp] = 0.0 if jp < j else 1.0
        in_maps.append(dict(xT=xTs[ci], wblk=wall, wga=wga, adaw=adaw, alw=alw, vecs=v, **consts))
    res = run_bass_kernel_spmd(_prog(), in_maps, core_ids=list(range(ncore)))
    out = np.empty((2, 4 * NTOK, D), np.float32)
    for ci in range(ncore):
        out[ci // 4, (ci % 4) * NTOK:(ci % 4 + 1) * NTOK, :] = np.asarray(res.results[ci]["xo"], np.float32).T
    return out
```

```python
import numpy as np
import concourse.bass as bass
import concourse.mybir as mybir
from concourse.bass_utils import run_bass_kernel_spmd

F32 = mybir.dt.float32
BF16 = mybir.dt.bfloat16
AF = mybir.ActivationFunctionType
ALU = mybir.AluOpType
AX = mybir.AxisListType

D = 1024
NTOK = 2048
TT = 512
NTILE = NTOK // TT
C = 64
NCH = TT // C
EPS = 1e-6
F_FLOOR = 1e-20
NBLK = 24
MG0 = 7184
BLK_COL0 = [512 * i for i in range(14)] + [MG0 + 512 * i for i in range(4)]
SW = 2048 + 12
V_C = 0
V_ADAB = 8
V_NW = 32
V_LB0 = 40
V_LB1 = 48
V_HNW = 56
V_ALB = 64
V_GNW = 68
V_FNW = 70
V_M = 78
V_OM = 82
NV = 86

ENGS = ("pe", "act", "dve", "pool", "sp")


class Sched:
    def __init__(self):
        self.ops = {e: [] for e in ENGS}
        self.cnt = {e: 0 for e in ENGS}
        self.lastw = {}
        self.readers = {}
        self.seen = {e: {} for e in ENGS}
        self.dma_rr = {e: 0 for e in ENGS}
        self.dma_val = {}
        self.ndsem = 6
        self.capture = None
        self.atom_depth = 0
        self.atom_open = False

    def atom_begin(self):
        self.atom_depth += 1
        self.atom_open = False

    def atom_end(self):
        self.atom_depth -= 1
        self.atom_open = False

    def replay_merged(self, A, B):
        na, nb = len(A), len(B)
        ia = ib = 0
        while ia < na or ib < nb:
            take_a = ib >= nb or (ia < na and ia * max(nb, 1) <= ib * max(na, 1))
            atom = A[ia] if take_a else B[ib]
            if take_a:
                ia += 1
            else:
                ib += 1
            for rec in atom:
                self.op(*rec)

    def op(self, eng, fn, reads=(), writes=(), dma=False, cc=False):
        if self.capture is not None:
            rec = (eng, fn, tuple(reads), tuple(writes), dma, cc)
            if self.atom_depth > 0 and self.capture and self.atom_open:
                self.capture[-1].append(rec)
            else:
                self.capture.append([rec])
                self.atom_open = self.atom_depth > 0
            return None
        waits = set()
        for r in reads:
            if r in self.lastw:
                waits.add(self.lastw[r])
        for w in writes:
            if w in self.lastw:
                waits.add(self.lastw[w])
            for ev in self.readers.get(w, ()):
                waits.add(ev)
        if cc:
            key = ("cc",)
            prev = self.dma_val.get(key, 0)
            ev = (key, prev + 1)
            self.dma_val[key] = prev + 1
        elif dma:
            key = ("d", eng, self.dma_rr[eng] % self.ndsem)
            self.dma_rr[eng] += 1
            prev = self.dma_val.get(key, 0)
            if prev:
                waits.add((key, prev))
            ev = (key, prev + 16)
            self.dma_val[key] = prev + 16
        else:
            self.cnt[eng] += 1
            ev = (eng, self.cnt[eng])
        need = {}
        for k, v in waits:
            need[k] = max(need.get(k, 0), v)
        wl = []
        for k, v in need.items():
            if k == "pe" and eng == "pe":
                continue
            if self.seen[eng].get(k, 0) >= v:
                continue
            self.seen[eng][k] = v
            wl.append((k, v))
        self.ops[eng].append((wl, fn, ev))
        for w in writes:
            self.lastw[w] = ev
            self.readers[w] = []
        for r in reads:
            self.readers.setdefault(r, []).append(ev)
        return ev

    def final_wait(self, eng, events):
        self.ops[eng].append(([e for e in events], None, None))


def build_program(phases):
    nc = bass.Bass("TRN2", target_bir_lowering=False)

    xT_d = nc.dram_tensor("xT", [D, NTOK], F32, kind="ExternalInput").ap()
    wblk_d = nc.dram_tensor("wblk", [2, NBLK, 128, 4096], F32, kind="ExternalInput").ap()
    wga_d = nc.dram_tensor("wga", [2, 128, 128], F32, kind="ExternalInput").ap()
    adaw_d = nc.dram_tensor("adaw", [2, 12, 128, 2048], F32, kind="ExternalInput").ap()
    vecs_d = nc.dram_tensor("vecs", [2, 128, NV], F32, kind="ExternalInput").ap()
    alw_d = nc.dram_tensor("alw", [2, 16, 512], F32, kind="ExternalInput").ap()
    ident_d = nc.dram_tensor("ident", [128, 128], F32, kind="ExternalInput").ap()
    ones_d = nc.dram_tensor("ones", [128, 128], F32, kind="ExternalInput").ap()
    causal_d = nc.dram_tensor("causal", [64, 64], F32, kind="ExternalInput").ap()
    rmask_d = nc.dram_tensor("rmask", [128, TT], F32, kind="ExternalInput").ap()
    xo_d = nc.dram_tensor("xo", [D, NTOK], F32, kind="ExternalOutput").ap()
    x1_d = nc.dram_tensor("x1s", [D, NTOK], F32).ap()
    cin_t = [nc.dram_tensor(f"cin{i}", [128, 1036], F32) for i in range(2)]
    cout_t = [nc.dram_tensor(f"cout{i}", [4 * 128, 1036], F32) for i in range(2)]
    cin_d = [t_.ap() for t_ in cin_t]
    cout_d = [t_.ap() for t_ in cout_t]

    S = Sched()
    import contextlib
    es = contextlib.ExitStack()

    def sb(name, shape, dt):
        return es.enter_context(nc.sbuf_tensor(name, shape, dt))

    def ps(name, shape, dt):
        return es.enter_context(nc.psum_tensor(name, shape, dt))

    with es:
        xts = [sb(f"xt{i}", [128, 8 * TT], F32) for i in range(2)]
        tile_n = [0]
        hTs = [sb(f"hT{i}", [128, 8 * TT], BF16) for i in range(2)]
        hcur = [None, None]
        NW = 4
        wbuf = [sb(f"wbuf{i}", [128, 4096], BF16) for i in range(NW)]
        wga = sb("wga_s", [128, 128], BF16)
        alw = sb("alw_s", [16, 512], F32)
        vecs = sb("vecs_s", [128, NV], F32)
        ident = sb("ident_s", [128, 128], BF16)
        ones = sb("ones_s", [128, 128], BF16)
        causal = sb("causal_s", [64, 64], F32)
        rmask = sb("rmask_s", [128, TT], BF16)
        mod = sb("mod", [128, 24], F32)
        cact = sb("cact", [128, 8], F32)
        nw1 = sb("nw1", [128, 8], F32)
        lb = sb("lb", [128, 8], F32)
        oml = sb("oml", [128, 8], F32)
        nalb = sb("nalb", [128, 4], F32)
        noml = sb("noml", [128, 8], F32)
        lbe = sb("lbe", [128, 8], F32)
        NSET = 2
        Tsets = [[sb(f"T{i}_{p}", [128, TT], F32) for i in range(6)] for p in range(NSET)]
        qts = [sb(f"qt_{p}", [128, TT], BF16) for p in range(NSET)]
        kts = [sb(f"kt_{p}", [128, TT], BF16) for p in range(NSET)]
        khs = [sb(f"kh_{p}", [128, TT], BF16) for p in range(NSET)]
        qes = [sb(f"qe_{p}", [128, TT], BF16) for p in range(NSET)]
        sms = [sb(f"sm_{p}", [128, 4 * NCH], F32) for p in range(NSET)]
        T = Tsets[0]
        sqb = [sb(f"sqb{i}", [128, TT], BF16) for i in range(2)]
        tmpx = [sb(f"tmpx{i}", [128, TT], F32) for i in range(2)]
        rstd = sb("rstd", [128, TT], F32)
        v_tms = [sb(f"v_tm{i}", [128, NCH * 512], BF16) for i in range(2)]
        mv_n = [0]
        khtms = [sb(f"khtm_{p}", [128, NCH * 128], BF16) for p in range(NSET)]
        state = sb("state", [128, 2048], F32)
        bsum = sb("bsum", [128, 12], F32)
        btmp = sb("btmp", [128, 1], F32)
        ga_sb = sb("ga_sb", [16, TT], F32)
        A_sbs = [sb(f"A_sb_{p}", [64, NCH * 64], BF16) for p in range(NSET)]
        Salls = [sb(f"Sall_{p}", [128, NCH * 256], F32) for p in range(NSET)]
        Sts = [sb(f"St_{p}", [128, NCH * 256], BF16) for p in range(NSET)]
        hs_n = [0]
        oy = sb("oy", [128, 16 * TT], BF16)
        ssacc = sb("ssacc", [128, TT], F32)
        merged = sb("merged", [128, 8 * TT], BF16)
        sdec = sb("sdec", [128, 12], F32)
        dm = sb("dm", [128, 12], F32)
        sfin = sb("sfin", [128, 12], F32)
        cst = sb("cst", [128, 3], F32)
        pp = [ps(f"pp{i}", [128, 512], F32) for i in range(3)]
        Pp = [ps(f"Pp{i}", [128, 512], F32) for i in range(2)]
        TR = ps("TR", [128, 1024], BF16)
        AT = ps("AT", [128, 512], F32)
        OT = ps("OT", [128, 512], F32)

        out_events = []
        setup_layer = [None]
        preload = {}

        def phase(mode, layer, last, prefetch_next=False):
            full = mode == "B"
            st = {"pp": 0, "w": 0}

            def v3(t, inner):
                return t[:].rearrange("p (c t) -> p c t", t=inner)

            def dma(eng, out, in_, reads, writes):
                return S.op(eng, lambda e: e.dma_start(out=out, in_=in_), reads, writes, dma=True)

            def act(out, in_, func, reads, writes, **kw):
                return S.op("act", lambda e: e.activation(out=out, in_=in_, func=func, **kw), reads, writes)

            def tt(out, in0, in1, op, reads, writes, eng="dve"):
                return S.op(eng, lambda e: e.tensor_tensor(out=out, in0=in0, in1=in1, op=op), reads, writes)

            def ts(out, in0, s1, s2, op0, op1, reads, writes, eng="dve"):
                if op1 is None:
                    return S.op(eng, lambda e: e.tensor_scalar(out=out, in0=in0, scalar1=s1, scalar2=None, op0=op0), reads, writes)
                return S.op(eng, lambda e: e.tensor_scalar(out=out, in0=in0, scalar1=s1, scalar2=s2, op0=op0, op1=op1), reads, writes)

            def stt(out, in0, scalar, in1, op0, op1, reads, writes):
                return S.op("dve", lambda e: e.scalar_tensor_tensor(out=out, in0=in0, scalar=scalar, in1=in1, op0=op0, op1=op1), reads, writes)

            def mm(out, lhsT, rhs, start, stop, reads, writes):
                return S.op("pe", lambda e: e.matmul(out, lhsT=lhsT, rhs=rhs, start=start, stop=stop), reads, writes)

            def next_pp():
                i = st["pp"] % len(pp)
                st["pp"] += 1
                return i

            wslot = {}

            def load_w(blk, i):
                wslot[blk] = i
                if preload.get(blk) == i:
                    del preload[blk]
                    return i
                dma("pool", wbuf[i][:], wblk_d[layer, blk], [], [f"w{i}"])
                return i

            def w3(blk):
                return wbuf[wslot[blk]][:].rearrange("p (k c) -> p k c", c=512), f"w{wslot[blk]}"

            hTs3 = [h_[:].rearrange("p (k t) -> p k t", t=TT) for h_ in hTs]
            prepped = {}
            xts3 = [x_[:].rearrange("p (k t) -> p k t", t=TT) for x_ in xts]
            loaded = {}
            tile_base = tile_n[0]
            tile_n[0] += NTILE

            def proj_fm(blk, g):
                w, wr = w3(blk)
                i = next_pp()
                for kc in range(8):
                    mm(pp[i][:], w[:, kc, g * 128:(g + 1) * 128], hcur[0][:, kc, :], kc == 0, kc == 7, [wr, hcur[1]], [f"pp{i}"])
                return i

            if setup_layer[0] != layer:
                dma("sp", vecs[:], vecs_d[layer], [], ["vecs"])
                dma("sp", alw[:], alw_d[layer], [], ["alw"])
                dma("pool", wga[:], wga_d[layer], [], ["wga"])
            need_setup = setup_layer[0] != layer
            setup_layer[0] = layer
            if need_setup:
                act(cact[:], vecs[:, V_C:V_C + 8], AF.Silu, ["vecs"], ["cact"])
            mi = next_pp()
            ada_slots = [(hTs[0][:].bitcast(F32), ["hT0"]), (hTs[1][:].bitcast(F32), ["hT1"]),
                         (oy[:, 0:4096].bitcast(F32), [f"oy{g_}" for g_ in range(8)]),
                         (oy[:, 4096:8192].bitcast(F32), [f"oy{g_}" for g_ in range(8, 16)])]
            for blk in range(12 if need_setup else 0):
                adab_ap, ares = ada_slots[blk % 4]
                ada3 = adab_ap.rearrange("p (k c) -> p k c", c=256)
                dma("sp", adab_ap, adaw_d[layer, blk], [], ares)
                for g in range(2):
                    j = blk * 2 + g
                    for kc in range(8):
                        mm(pp[mi][:, j:j + 1], ada3[:, kc, g * 128:(g + 1) * 128], cact[:, kc:kc + 1], kc == 0, kc == 7,
                           ares + ["cact"], [f"pp{mi}"])
            if need_setup:
                tt(mod[:, 0:24], pp[mi][:, 0:24], vecs[:, V_ADAB:V_ADAB + 24], ALU.add, [f"pp{mi}", "vecs"], ["mod"])
                stt(nw1[:], mod[:, 8:16], 1.0, vecs[:, V_NW:V_NW + 8], ALU.add, ALU.mult, ["mod", "vecs"], ["nw1"])
            if not need_setup:
                pass
            elif layer == 0:
                S.op("dve", lambda e: e.memset(lb[:], 0.0), [], ["lb"])
                S.op("dve", lambda e: e.memset(oml[:], 1.0), [], ["oml"])
            else:
                tt(lb[:], vecs[:, V_LB1:V_LB1 + 8], vecs[:, V_LB0:V_LB0 + 8], ALU.subtract, ["vecs"], ["lb"])
                act(lb[:], lb[:], AF.Sigmoid, ["lb"], ["lb"])
                ts(oml[:], lb[:], -1.0, 1.0, ALU.mult, ALU.add, ["lb"], ["oml"])
            if need_setup:
                ts(nalb[:], vecs[:, V_ALB:V_ALB + 4], -1.0, None, ALU.mult, None, ["vecs"], ["nalb"])
                ts(noml[:], oml[:], -1.0, None, ALU.mult, None, ["oml"], ["noml"])
                ts(lbe[:], lb[:], F_FLOOR, None, ALU.add, None, ["lb"], ["lbe"])
            S.op("dve", lambda e: e.memset(bsum[:], 0.0), [], ["bsum"])
            S.op("dve", lambda e: e.memset(sfin[:], 0.0), [], ["sfin0", "sfin1"])
            init_done = [False, False]

            def init_half(i):
                if init_done[i]:
                    return
                init_done[i] = True
                c_lo, c_hi = i * 1024, (i + 1) * 1024
                d0, nd = (0, 8) if i == 0 else (8, 4)
                S.op("dve", lambda e: e.memset(state[:, c_lo:c_hi], 0.0), [], ["state"])
                if full:
                    sblk = merged[:].bitcast(F32)
                    MG = [f"mg{d_}" for d_ in range(4 * i, 4 * i + 4)]
                    for jp in range(4):
                        dma("sp", sblk[:, c_lo:c_hi], cout_d[i][jp * 128:(jp + 1) * 128, 0:1024], [f"cout{i}"], MG)
                        dma("sp", sdec[:, d0:d0 + nd], cout_d[i][jp * 128:(jp + 1) * 128, 1024 + d0:1024 + d0 + nd], [f"cout{i}"], [f"sdec{i}"])
                        ts(dm[:, d0:d0 + nd], sdec[:, d0:d0 + nd], vecs[:, V_M + jp:V_M + jp + 1], vecs[:, V_OM + jp:V_OM + jp + 1],
                           ALU.mult, ALU.add, [f"sdec{i}", "vecs"], [f"dm{i}"])
                        for h in range(d0, d0 + nd):
                            c0, c1 = (h * 128, (h + 1) * 128) if h < 8 else (1024 + (h - 8) * 256, 1024 + (h - 7) * 256)
                            ts(state[:, c0:c1], state[:, c0:c1], dm[:, h:h + 1], None, ALU.mult, None, ["state", f"dm{i}"], ["state"])
                        stt(state[:, c_lo:c_hi], sblk[:, c_lo:c_hi], vecs[:, V_M + jp:V_M + jp + 1], state[:, c_lo:c_hi], ALU.mult, ALU.add,
                            MG + ["vecs", "state"], ["state"])

            def exchange(i):
                d0, nd = (0, 8) if i == 0 else (8, 4)
                act(sfin[:, d0:d0 + nd], bsum[:, d0:d0 + nd], AF.Exp, ["bsum"], [f"sfin{i}"], scale=(1.0 if i == 0 else -1.0 / 16.0))
                dma("pool", cin_d[i][:, 0:1024], state[:, i * 1024:(i + 1) * 1024], ["state", f"cout{i}"], [f"cin{i}"])
                dma("pool", cin_d[i][:, 1024:1036], sfin[:, 0:12], ["sfin0", "sfin1"], [f"cin{i}"])
                S.op("pool", lambda e: e.collective_compute("AllGather", ALU.bypass, replica_groups=[[0, 1, 2, 3], [4, 5, 6, 7]],
                                                            ins=[cin_t[i].ap().opt()], outs=[cout_t[i].ap().opt()]),
                     [f"cin{i}"], [f"cout{i}"], cc=True)

            def rstd_from(psrc, reads, inv_n):
                act(rstd[:], psrc, AF.Ln, list(reads) + ["cst"], ["rstd"], scale=inv_n, bias=vecs_eps)
                act(rstd[:], rstd[:], AF.Exp, ["rstd"], ["rstd"], scale=-0.5)

            def sumsq_rows(srcs, reads_list, inv_n):
                i = next_pp()
                n = len(srcs)
                for k, (src, rd) in enumerate(zip(srcs, reads_list)):
                    b = k % 2
                    act(sqb[b][:], src, AF.Square, rd, [f"sqb{b}"])
                    mm(pp[i][:], ones[:], sqb[b][:], k == 0, k == n - 1, ["ones", f"sqb{b}"], [f"pp{i}"])
                rstd_from(pp[i][:], [f"pp{i}"], inv_n)

            def load_x(t_):
                if t_ in loaded or t_ >= NTILE:
                    return
                p_ = (tile_base + t_) % 2
                loaded[t_] = p_
                dma("sp", xts3[p_], (xT_d if layer == 0 else x1_d).rearrange("(k p) t -> p k t", p=128)[:, :, t_ * TT:(t_ + 1) * TT],
                    [] if layer == 0 else [f"x1_{t_}"], [f"xt{p_}"])

            def prep_h(t_):
                if t_ in prepped or t_ >= NTILE:
                    return
                prepped[t_] = True
                p_ = (tile_base + t_) % 2
                x3, XR_ = xts3[p_], f"xt{p_}"
                load_x(t_)
                sumsq_rows([x3[:, kc, :] for kc in range(8)], [[XR_]] * 8, 1.0 / D)
                for kc in range(8):
                    b = kc % 2
                    tt(tmpx[b][:], x3[:, kc, :], rstd[:], ALU.mult, [XR_, "rstd"], [f"tmpx{b}"])
                    act(hTs3[p_][:, kc, :], tmpx[b][:], AF.Identity, [f"tmpx{b}", "nw1", "mod"], [f"hT{p_}"],
                        scale=nw1[:, kc:kc + 1], bias=mod[:, kc:kc + 1])

            def make_hT(t):
                xp = (tile_base + t) % 2
                prep_h(t)
                hcur[0], hcur[1] = hTs3[xp], f"hT{xp}"
                load_x(t + 1)
                return xts3[xp], f"xt{xp}"

            def make_v(blk):
                if not full:
                    return make_v_A(blk)
                w, wr = w3(blk)
                vi = mv_n[0] % 2
                mv_n[0] += 1
                vt3 = v_tms[vi][0:64, :].rearrange("p (c n) -> p c n", n=512)
                for tsub in range(TT // 128):
                    i = next_pp()
                    for kc in range(8):
                        mm(pp[i][:], hcur[0][:, kc, tsub * 128:(tsub + 1) * 128], w[:, kc, :], kc == 0, kc == 7, [wr, hcur[1]], [f"pp{i}"])
                    act(vt3[:, 2 * tsub, :], pp[i][0:64, :], AF.Copy, [f"pp{i}"], [f"v_tm{vi}"])
                    S.op("dve", lambda e, o=vt3[:, 2 * tsub + 1, :], a=pp[i][64:128, :]: e.tensor_copy(out=o, in_=a), [f"pp{i}"], [f"v_tm{vi}"])
                return vi

            def make_v_A(blk):
                w, wr = w3(blk)
                vi = mv_n[0] % 2
                mv_n[0] += 1
                vtA = v_tms[vi][:, 0:2048].rearrange("p (c n) -> p c n", n=512)
                for tsub in range(TT // 128):
                    i = next_pp()
                    for kc in range(8):
                        mm(pp[i][:], hcur[0][:, kc, tsub * 128:(tsub + 1) * 128], w[:, kc, :], kc == 0, kc == 7, [wr, hcur[1]], [f"pp{i}"])
                    if tsub % 2 == 0:
                        act(vtA[:, tsub, :], pp[i][:], AF.Copy, [f"pp{i}"], [f"v_tm{vi}"])
                    else:
                        S.op("dve", lambda e, o=vtA[:, tsub, :], a=pp[i][:]: e.tensor_copy(out=o, in_=a), [f"pp{i}"], [f"v_tm{vi}"])
                return vi

            def head_scan_A(hidx, V, vcol0, vi, sc, cum_src_fn):
                scol0 = hidx * 128 if hidx < 8 else 1024 + (hidx - 8) * 256
                p = hs_n[0] % NSET
                hs_n[0] += 1
                T, kh, sm, khtm = Tsets[p], khs[p], sms[p], khtms[p]

                def N(nm):
                    return f"{nm}_{p}"
                cum_src_fn(T, N)
                S.op("dve", lambda e: e.tensor_tensor_scan(out=T[4][:], data0=cst[:, 1:2].broadcast_to([128, TT]), data1=T[3][:],
                                                            initial=0.0, op0=ALU.mult, op1=ALU.add), ["cst", N("T3")], [N("T4")])
                bl = T[4][:, TT - 1:TT]
                act(sm[:, 0:1], bl, AF.Exp, [N("T4")], [N("El")], scale=sc)
                tt(bsum[:, hidx:hidx + 1], bsum[:, hidx:hidx + 1], bl, ALU.add, [N("T4"), "bsum"], ["bsum"])
                ts(T[5][:], T[4][:], bl, None, ALU.subtract, None, [N("T4")], [N("T5")])
                act(T[3][:], T[5][:], AF.Exp, [N("T5")], [N("T3")], scale=-sc)
                tt(kh[:], T[2][:], T[3][:], ALU.mult, [N("T2"), N("T3")], [N("kh")])
                yield
                TR3 = TR[:].rearrange("p (c k) -> p c k", k=128)
                for c in range(TT // 128):
                    S.op("pe", lambda e, c=c: e.transpose(out=TR3[:, c, :], in_=kh[:, c * 128:(c + 1) * 128], identity=ident[:]),
                         [N("kh"), "ident"], ["TR"])
                act(khtm[:, 0:TT], TR[:, 0:TT], AF.Copy, ["TR"], [N("khtm")])
                khA = khtm[:, 0:TT].rearrange("p (c k) -> p c k", k=128)
                vtA = v_tms[vi][:, 0:2048].rearrange("p (c n) -> p c n", n=512)
                nsub = TT // 128
                for c in range(nsub):
                    mm(Pp[p][:, 0:V], khA[:, c, :], vtA[:, c, vcol0:vcol0 + V], c == 0, c == nsub - 1,
                       [N("khtm"), f"v_tm{vi}"], [f"Pp{p}"])
                stt(state[:, scol0:scol0 + V], state[:, scol0:scol0 + V], sm[:, 0:1], Pp[p][:, 0:V], ALU.mult, ALU.add,
                    ["state", N("El"), f"Pp{p}"], ["state"])
                return None

            def head_scan(hidx, V, vcol0, vi, sc, cum_src_fn):
                if not full:
                    return (yield from head_scan_A(hidx, V, vcol0, vi, sc, cum_src_fn))
                scol0 = hidx * 128 if hidx < 8 else 1024 + (hidx - 8) * 256
                p = hs_n[0] % NSET
                hs_n[0] += 1
                T, qt, kt, kh, qe, sm = Tsets[p], qts[p], kts[p], khs[p], qes[p], sms[p]
                A_sb, khtm, Sall, St = A_sbs[p], khtms[p], Salls[p], Sts[p]

                def N(nm):
                    return f"{nm}_{p}"
                cum_src_fn(T, N)
                S.op("dve", lambda e: e.tensor_tensor_scan(out=T[4][:], data0=rmask[:], data1=T[3][:], initial=0.0,
                                                            op0=ALU.mult, op1=ALU.add), ["rmask", N("T3")], [N("T4")])
                b3 = v3(T[4], C)
                sm3 = sm[:].rearrange("p (a c) -> p a c", c=NCH)
                act(sm3[:, 3, :].unsqueeze(2), b3[:, :, C - 1:C], AF.Exp, [N("T4")], [N("El")], scale=sc)
                S.op("dve", lambda e: e.tensor_reduce(out=btmp[:], in_=b3[:, :, C - 1:C],
                                                      axis=AX.XY, op=ALU.add), [N("T4")], ["btmp"])
                tt(bsum[:, hidx:hidx + 1], bsum[:, hidx:hidx + 1], btmp[:], ALU.add, ["btmp", "bsum"], ["bsum"])
                if not full:
                    tt(v3(T[5], C), b3, b3[:, :, C - 1:C].broadcast_to([128, NCH, C]), ALU.subtract, [N("T4")], [N("T5")])
                    act(T[3][:], T[5][:], AF.Exp, [N("T5")], [N("T3")], scale=-sc)
                    tt(kh[:], T[2][:], T[3][:], ALU.mult, [N("T2"), N("T3")], [N("kh")])
                    yield
                else:
                    ref = C // 2 - 1
                    tt(v3(T[5], C), b3, b3[:, :, ref:ref + 1].broadcast_to([128, NCH, C]), ALU.subtract, [N("T4")], [N("T5")])
                    act(T[1][:], T[5][:], AF.Exp, [N("T5")], [N("T1")], scale=-sc)
                    tt(sm3[:, 0, :].unsqueeze(2), b3[:, :, C - 1:C], b3[:, :, ref:ref + 1], ALU.subtract, [N("T4")], [N("blm")])
                    act(sm3[:, 1, :], sm3[:, 0, :], AF.Exp, [N("blm")], [N("Dk")], scale=sc)
                    tt(kt[:], T[2][:], T[1][:], ALU.mult, [N("T2"), N("T1")], [N("kt")])
                    tt(v3(kh, C), v3(kt, C), sm3[:, 1, :].unsqueeze(2).broadcast_to([128, NCH, C]), ALU.mult, [N("kt"), N("Dk")], [N("kh")])
                    act(T[3][:], T[5][:], AF.Exp, [N("T5")], [N("T3")], scale=sc)
                    tt(qt[:], T[0][:], T[3][:], ALU.mult, [N("T0"), N("T3")], [N("qt")])
                    act(T[4][:], T[4][:], AF.Exp, [N("T4")], [N("T4")], scale=sc)
                    tt(qe[:], T[0][:], T[4][:], ALU.mult, [N("T0"), N("T4")], [N("qe")])
                    yield
                TR3 = TR[:].rearrange("p (c k) -> p c k", k=128)
                for c in range(NCH):
                    S.op("pe", lambda e, c=c: e.transpose(out=TR3[0:C, c, :], in_=kh[:, c * C:(c + 1) * C], identity=ident[:]),
                         [N("kh"), "ident"], ["TR"])
                act(khtm[0:C, :], TR[0:C, :], AF.Copy, ["TR"], [N("khtm")])
                if full:
                    AT3 = AT[:].rearrange("p (c t) -> p c t", t=C)
                    h2 = C // 2
                    for c in range(NCH):
                        mm(AT3[0:C, c, h2:C], kt[:, c * C:(c + 1) * C], qt[:, c * C + h2:(c + 1) * C], True, True, [N("kt"), N("qt")], ["AT"])
                        mm(AT3[0:h2, c, 0:h2], kt[:, c * C:c * C + h2], qt[:, c * C:c * C + h2], True, True, [N("kt"), N("qt")], ["AT"])
                    tt(v3(A_sb, C), AT3[0:C, :, :], causal[:].unsqueeze(1).broadcast_to([C, NCH, C]), ALU.mult, ["AT", "causal"], [N("A_sb")])
                kh3 = khtm[0:C, :].rearrange("p (c k) -> p c k", k=128)
                vt3 = v_tms[vi][0:C, :].rearrange("p (c n) -> p c n", n=512)
                per = 512 // V
                if full:
                    Sall3 = Sall[:].rearrange("p (c v) -> p c v", v=256)
                    St3 = St[:].rearrange("p (c v) -> p c v", v=256)

                def S_at(c):
                    if c == 0 or c == NCH:
                        return state[:, scol0:scol0 + V], "state"
                    return Sall3[:, c, 0:V], N(f"Sall{c}")

                for c in range(NCH):
                    pi = (c // per) % 2
                    po = (c % per) * V
                    if c % (2 * per) == 0:
                        for c2 in range(c, min(c + 2 * per, NCH)):
                            pi2 = (c2 // per) % 2
                            po2 = (c2 % per) * V
                            mm(Pp[pi2][:, po2:po2 + V], kh3[:, c2, :], vt3[:, c2, vcol0:vcol0 + V], True, True,
                               [N("khtm"), f"v_tm{vi}"], [f"Pp{pi2}"])
                    if full:
                        src, sr = S_at(c)
                        act(St3[:, c, 0:V], src, AF.Copy, [sr], [N(f"St{c}")])
                        dst, dr = S_at(c + 1)
                        stt(dst, src, sm3[:, 3, c:c + 1], Pp[pi][:, po:po + V], ALU.mult, ALU.add,
                            [sr, N("El"), f"Pp{pi}"], [dr])
                    else:
                        stt(state[:, scol0:scol0 + V], state[:, scol0:scol0 + V], sm3[:, 3, c:c + 1], Pp[pi][:, po:po + V],
                            ALU.mult, ALU.add, ["state", N("El"), f"Pp{pi}"], ["state"])
                if full:
                    outs = []
                    for vg in range(V // 128):
                        for c in range(NCH):
                            mm(OT[:, c * C:(c + 1) * C], vt3[:, c, vcol0 + vg * 128:vcol0 + (vg + 1) * 128], v3(A_sb, C)[:, c, :],
                               True, False, [f"v_tm{vi}", N("A_sb")], ["OT"])
                            mm(OT[:, c * C:(c + 1) * C], St3[:, c, vg * 128:(vg + 1) * 128], qe[:, c * C:(c + 1) * C],
                               False, True, [N(f"St{c}"), N("qe")], ["OT"])
                        gi = (hidx if hidx < 8 else 8 + (hidx - 8) * 2 + vg)
                        o = oy[:, gi * TT:(gi + 1) * TT]
                        b = gi % 2
                        act(sqb[b][:], OT[:], AF.Square, ["OT"], [f"sqb{b}", "OT_rd"])
                        S.op("dve", lambda e, o=o: e.tensor_copy(out=o, in_=OT[:]), ["OT"], [f"oy{gi}", "OT_rd"])
                        outs.append((gi, b))
                    return outs
                return None

            def hgrn_src(blk_f, g, hh, blk_q):
                def fn(T, N):
                    if full:
                        S.atom_begin()
                        iq = proj_fm(blk_q, g)
                        act(T[0][:], pp[iq][:], AF.Exp, [f"pp{iq}"], [N("T0")], scale=-1.0)
                        act(T[0][:], T[0][:], AF.Ln, [N("T0"), "cst"], [N("T0")], bias=vecs_one)
                        act(T[0][:], T[0][:], AF.Exp, [N("T0")], [N("T0")], scale=-1.0)
                        tt(T[0][:], T[0][:], pp[iq][:], ALU.mult, [N("T0"), f"pp{iq}"], [N("T0")])
                        S.atom_end()
                    S.atom_begin()
                    i = proj_fm(blk_f, g)
                    act(T[1][:], pp[i][:], AF.Exp, [f"pp{i}"], [N("T1")], scale=-1.0)
                    S.atom_end()
                    act(T[1][:], T[1][:], AF.Ln, [N("T1"), "cst"], [N("T1")], bias=vecs_one)
                    act(T[1][:], T[1][:], AF.Exp, [N("T1")], [N("T1")], scale=-1.0)
                    ts(T[2][:], T[1][:], noml[:, hh:hh + 1], oml[:, hh:hh + 1], ALU.mult, ALU.add, [N("T1"), "oml", "noml"], [N("T2")])
                    act(T[3][:], T[1][:], AF.Ln, [N("T1"), "oml", "lbe"], [N("T3")], scale=oml[:, hh:hh + 1], bias=lbe[:, hh:hh + 1])
                return fn

            def gla_src(gh):
                def fn(T, N):
                    if full:
                        S.atom_begin()
                        iq = proj_fm(8, gh)
                        act(T[0][:], pp[iq][:], AF.Identity, [f"pp{iq}"], [N("T0")], scale=128.0 ** -0.5)
                        S.atom_end()
                    S.atom_begin()
                    ik = proj_fm(9, gh)
                    act(T[2][:], pp[ik][:], AF.Copy, [f"pp{ik}"], [N("T2")])
                    S.atom_end()
                    S.atom_begin()
                    iz = next_pp()
                    mm(pp[iz][:], alw[:, gh * 128:(gh + 1) * 128], ga_sb[:], True, True, ["alw", "ga_sb"], [f"pp{iz}"])
                    act(T[1][:], pp[iz][:], AF.Exp, [f"pp{iz}", "nalb"], [N("T1")], scale=-1.0, bias=nalb[:, gh:gh + 1])
                    S.atom_end()
                    act(T[3][:], T[1][:], AF.Ln, [N("T1"), "cst"], [N("T3")], bias=vecs_one)
                return fn

            vecs_eps = cst[:, 0:1]
            vecs_one = cst[:, 1:2]

            for t in range(NTILE):
                xt3, XR = make_hT(t)
                def fin(gen):
                    try:
                        next(gen)
                    except StopIteration as e_:
                        return e_.value
                    raise RuntimeError("head_scan did not finish")

                def fin_hgrn(pend):
                    gen, hh = pend
                    init_half(0)
                    res = fin(gen)
                    if full:
                        (gi, b), = res
                        S.atom_begin()
                        i = next_pp()
                        mm(pp[i][:], ones[:], sqb[b][:], True, True, ["ones", f"sqb{b}"], [f"pp{i}"])
                        if hh == 0:
                            S.op("dve", lambda e, i=i: e.tensor_copy(out=ssacc[:], in_=pp[i][:]), [f"pp{i}"], ["ssacc"])
                        else:
                            tt(ssacc[:], ssacc[:], pp[i][:], ALU.add, [f"pp{i}", "ssacc"], ["ssacc"])
                        S.atom_end()

                def pipelined(gen, fin_fn, pend):
                    S.capture = []
                    next(gen)
                    A = S.capture
                    S.capture = []
                    if pend is not None:
                        fin_fn(pend)
                    B = S.capture
                    S.capture = None
                    S.replay_merged(A, B)

                pend = None
                load_w(4, 2)
                load_w(5, 3)
                if full:
                    load_w(0, 0)
                load_w(2, 1)
                vis = [make_v(4), make_v(5)]
                if full:
                    load_w(1, 2)
                load_w(3, 3)
                for qd in range(2):
                    vi = vis[qd]
                    for g in range(4):
                        hh = qd * 4 + g
                        gen = head_scan(hh, 128, g * 128, vi, 1.0, hgrn_src(2 + qd, g, hh, qd))
                        pipelined(gen, fin_hgrn, pend)
                        pend = (gen, hh)
                fin_hgrn(pend)
                if full:
                    rstd_from(ssacc[:], ["ssacc"], 1.0 / 1024)
                    for g in range(8):
                        if g % 4 == 0:
                            load_w(6 + g // 4, 0 if g == 0 else 1)
                        i = proj_fm(6 + g // 4, g % 4)
                        act(T[0][:], pp[i][:], AF.Silu, [f"pp{i}"], ["T0_0"])
                        o = oy[:, g * TT:(g + 1) * TT]
                        tt(T[1][:], o, rstd[:], ALU.mult, [f"oy{g}", "rstd"], ["T1_0"])
                        stt(o, T[1][:], vecs[:, V_HNW + g:V_HNW + g + 1], T[0][:], ALU.mult, ALU.mult, ["T1_0", "T0_0", "vecs"], [f"oy{g}"])
                if not full:
                    prep_h(t + 1)
                if full:
                    load_w(8, 2)
                load_w(9, 3)
                load_w(10, 0)
                load_w(11, 1)
                if not full and t == NTILE - 1:
                    exchange(0)
                i = next_pp()
                for kc in range(8):
                    mm(pp[i][0:16, :], wga[:, kc * 16:(kc + 1) * 16], hcur[0][:, kc, :], kc == 0, kc == 7, ["wga", hcur[1]], [f"pp{i}"])
                act(ga_sb[:], pp[i][0:16, :], AF.Copy, [f"pp{i}"], ["ga_sb"])
                def fin_gla(pend):
                    gen, gh = pend
                    init_half(1)
                    res = fin(gen)
                    if full:
                        S.atom_begin()
                        i = next_pp()
                        for k, (gi, b) in enumerate(res):
                            mm(pp[i][:], ones[:], sqb[b][:], k == 0, k == 1, ["ones", f"sqb{b}"], [f"pp{i}"])
                        rstd_from(pp[i][:], [f"pp{i}"], 1.0 / 256)
                        S.atom_end()
                        for vg, (gi, b) in enumerate(res):
                            S.atom_begin()
                            iz = proj_fm(12 + gh // 2, (gh % 2) * 2 + vg)
                            act(tmpx[0][:], pp[iz][:], AF.Silu, [f"pp{iz}"], ["tmpx0"])
                            S.atom_end()
                            o = oy[:, gi * TT:(gi + 1) * TT]
                            tt(tmpx[1][:], o, rstd[:], ALU.mult, [f"oy{gi}", "rstd"], ["tmpx1"])
                            stt(o, tmpx[1][:], vecs[:, V_GNW + vg:V_GNW + vg + 1], tmpx[0][:], ALU.mult, ALU.mult, ["tmpx1", "tmpx0", "vecs"], [f"oy{gi}"])

                pend = None
                gvis = [make_v(10), make_v(11)]
                if full:
                    load_w(12, 0)
                    load_w(13, 1)
                for gh in range(4):
                    vi = gvis[gh // 2]
                    gen = head_scan(8 + gh, 256, (gh % 2) * 256, vi, -1.0 / 16.0, gla_src(gh))
                    pipelined(gen, fin_gla, pend)
                    pend = (gen, gh)
                fin_gla(pend)
                if not full:
                    continue
                mg3 = merged[:].rearrange("p (k t) -> p k t", t=TT)
                for half in range(2):
                    load_w(14 + half, 2)
                    load_w(16 + half, 3)
                    load_w(18 + half, 0)
                    load_w(20 + half, 1)
                    for g in range(4):
                        dg = half * 4 + g
                        i0 = proj_fm(14 + half, g)
                        act(T[0][:], pp[i0][:], AF.Sigmoid, [f"pp{i0}"], ["T0_0"])
                        i1 = proj_fm(16 + half, g)
                        act(T[1][:], pp[i1][:], AF.Sigmoid, [f"pp{i1}"], ["T1_0"])
                        for br, tgt, gate_t in ((0, 2, 0), (1, 3, 1)):
                            w, wr = w3(18 + 2 * br + half)
                            iu = next_pp()
                            for k in range(8):
                                mm(pp[iu][:], w[:, k, g * 128:(g + 1) * 128], oy[:, (br * 8 + k) * TT:(br * 8 + k + 1) * TT],
                                   k == 0, k == 7, [wr, f"oy{br * 8 + k}"], [f"pp{iu}"])
                            tt(T[tgt][:], pp[iu][:], T[gate_t][:], ALU.mult, [f"pp{iu}", f"T{gate_t}_0"], [f"T{tgt}_0"])
                        tt(mg3[:, dg, :], T[2][:], T[3][:], ALU.add, ["T2_0", "T3_0"], [f"mg{dg}"])
                prep_h(t + 1)
                for half in range(2):
                    load_w(22 + half, half)
                    w, wr = w3(22 + half)
                    for g in range(4):
                        dg = half * 4 + g
                        io = next_pp()
                        for k in range(8):
                            mm(pp[io][:], w[:, k, g * 128:(g + 1) * 128], mg3[:, k, :], k == 0, k == 7, [wr, f"mg{k}"], [f"pp{io}"])
                        stt(xt3[:, dg, :], pp[io][:], mod[:, 16 + dg:17 + dg], xt3[:, dg, :], ALU.mult, ALU.add,
                            [f"pp{io}", "mod", XR], [XR])
                if last:
                    sumsq_rows([xt3[:, kc, :] for kc in range(8)], [[XR]] * 8, 1.0 / D)
                    for kc in range(8):
                        tt(tmpx[kc % 2][:], xt3[:, kc, :], rstd[:], ALU.mult, [XR, "rstd"], [f"tmpx{kc % 2}"])
                        ts(xt3[:, kc, :], tmpx[kc % 2][:], vecs[:, V_FNW + kc:V_FNW + kc + 1], None, ALU.mult, None,
                           [f"tmpx{kc % 2}", "vecs"], [XR])
                if last:
                    ev = dma("sp", xo_d.rearrange("(k p) t -> p k t", p=128)[:, :, t * TT:(t + 1) * TT], xt3, [XR], ["xo"])
                    out_events.append(ev)
                else:
                    dma("sp", x1_d.rearrange("(k p) t -> p k t", p=128)[:, :, t * TT:(t + 1) * TT], xt3, [XR], [f"x1_{t}"])

            if not full:
                if prefetch_next:
                    for blk_, sl_ in ((4, 2), (5, 3), (0, 0), (2, 1)):
                        load_w(blk_, sl_)
                        preload[blk_] = sl_
                exchange(1)

        S.op("sp", lambda e: e.dma_start(out=causal[:], in_=causal_d[:, :]), [], ["causal"], dma=True)
        S.op("pool", lambda e: e.dma_start(out=rmask[:], in_=rmask_d[:, :]), [], ["rmask"], dma=True)
        S.op("pool", lambda e: e.dma_start(out=ident[:], in_=ident_d[:, :]), [], ["ident"], dma=True)
        S.op("pool", lambda e: e.dma_start(out=ones[:], in_=ones_d[:, :]), [], ["ones"], dma=True)
        S.op("dve", lambda e: e.memset(cst[:, 0:1], EPS), [], ["cst"])
        S.op("dve", lambda e: e.memset(cst[:, 1:2], 1.0), [], ["cst"])
        S.op("dve", lambda e: e.memset(cst[:, 2:3], 0.0), [], ["cst"])
        S.op("dve", lambda e: e.memset(AT[:], 0.0), [], ["AT"])
        for i, (mode, layer) in enumerate(phases):
            nxt = phases[i + 1] if i + 1 < len(phases) else None
            phase(mode, layer, i == len(phases) - 1, prefetch_next=(mode == "A" and nxt == ("B", layer)))
        S.final_wait("sp", out_events)

        sem = {}
        for e in ENGS:
            sem[e] = es.enter_context(nc.semaphore(f"s_{e}"))
        sem[("cc",)] = es.enter_context(nc.semaphore("s_cc"))
        for e in ("sp", "pool"):
            for i in range(S.ndsem):
                sem[("d", e, i)] = es.enter_context(nc.semaphore(f"d_{e}{i}"))
        block = es.enter_context(nc.Block())

        def emit(name, eng):
            for wl, fn, ev in S.ops[name]:
                for k, v in wl:
                    eng.wait_ge(sem[k], v)
                if fn is None:
                    continue
                inst = fn(eng)
                k, v = ev
                inst.then_inc(sem[k], 16 if (isinstance(k, tuple) and k[0] == "d") else 1)

        @block.tensor
        def _(e):
            emit("pe", e)

        @block.scalar
        def _(e):
            emit("act", e)

        @block.vector
        def _(e):
            emit("dve", e)

        @block.gpsimd
        def _(e):
            emit("pool", e)

        @block.sync
        def _(e):
            emit("sp", e)
    return nc


def _pm(v):
    v = np.asarray(v, np.float32)
    return np.ascontiguousarray(v.reshape(-1, 128).T)


def _blk(w, c0, n=512):
    b = w[:, c0:c0 + n].reshape(8, 128, n).transpose(1, 0, 2)
    return np.ascontiguousarray(b.reshape(128, 8 * n))


_PROG = {}


def _prog():
    if "p" not in _PROG:
        _PROG["p"] = build_program([("A", 0), ("B", 0), ("A", 1), ("B", 1)])
    return _PROG["p"]


def kernel(x, c, ada_w, ada_b, norm_w, w_in, hgrn_lb_logits, hgrn_norm_w, gla_alpha_w, gla_alpha_b, gla_norm_w,
           w_branch, w_out, final_norm_w):
    x = np.asarray(x, np.float32)
    ncore = 8
    consts = {
        "ident": np.eye(128, dtype=np.float32),
        "ones": np.ones((128, 128), np.float32),
        "causal": np.triu(np.ones((64, 64), np.float32)),
        "rmask": np.tile((np.arange(TT) % C != 0).astype(np.float32)[None, :], (128, 1)),
    }
    xTs = [np.ascontiguousarray(x[ci // 4, (ci % 4) * NTOK:(ci % 4 + 1) * NTOK, :].T) for ci in range(ncore)]
    walls, wgas, adaws, alws = [], [], [], []
    for l in range(2):
        wi = np.asarray(w_in[l], np.float32)
        blocks = [_blk(wi, c0) for c0 in BLK_COL0]
        blocks += [_blk(np.asarray(w_branch[l, 0], np.float32), 0), _blk(np.asarray(w_branch[l, 0], np.float32), 512)]
        blocks += [_blk(np.asarray(w_branch[l, 1], np.float32), 0), _blk(np.asarray(w_branch[l, 1], np.float32), 512)]
        blocks += [_blk(np.asarray(w_out[l], np.float32), 0), _blk(np.asarray(w_out[l], np.float32), 512)]
        walls.append(np.stack(blocks))
        wgas.append(_blk(wi, 7168, 16))
        adaws.append(np.stack([_blk(np.asarray(ada_w[l], np.float32), 256 * i, 256) for i in range(12)]))
        alws.append(np.asarray(gla_alpha_w[l], np.float32))
    wall = np.ascontiguousarray(np.stack(walls))
    wga = np.ascontiguousarray(np.stack(wgas))
    adaw = np.ascontiguousarray(np.stack(adaws))
    alw = np.ascontiguousarray(np.stack(alws))
    in_maps = []
    for ci in range(ncore):
        b, j = ci // 4, ci % 4
        v = np.zeros((2, 128, NV), np.float32)
        for l in range(2):
            v[l, :, V_C:V_C + 8] = _pm(c[b])
            v[l, :, V_ADAB:V_ADAB + 24] = _pm(ada_b[l])
            v[l, :, V_NW:V_NW + 8] = _pm(norm_w[l])
            v[l, :, V_LB0:V_LB0 + 8] = _pm(hgrn_lb_logits[0])
            v[l, :, V_LB1:V_LB1 + 8] = _pm(hgrn_lb_logits[1])
            v[l, :, V_HNW:V_HNW + 8] = _pm(hgrn_norm_w[l])
            v[l, :, V_ALB:V_ALB + 4] = _pm(gla_alpha_b[l])
            v[l, :, V_GNW:V_GNW + 2] = _pm(gla_norm_w[l])
            v[l, :, V_FNW:V_FNW + 8] = _pm(final_norm_w)
            for jp in range(4):
                v[l, :, V_M + jp] = 1.0 if jp < j else 0.0
                v[l, :, V_OM + jp] = 0.0 if jp < j else 1.0
        in_maps.append(dict(xT=xTs[ci], wblk=wall, wga=wga, adaw=adaw, alw=alw, vecs=v, **consts))
    res = run_bass_kernel_spmd(_prog(), in_maps, core_ids=list(range(ncore)))
    out = np.empty((2, 4 * NTOK, D), np.float32)
    for ci in range(ncore):
        out[ci // 4, (ci % 4) * NTOK:(ci % 4 + 1) * NTOK, :] = np.asarray(res.results[ci]["xo"], np.float32).T
    return out
```
